# Optimizing a Trainium2 kernel written in Bass

```python
import jax, jax.numpy as jnp
from jax import lax
import numpy as np

D_MODEL = 1024
BATCH = 16
SEQ = 256
DEPTH = 4
DEC_BATCH = 8
DEC_SEQ = 4096
PAST_LEN = 256

GRID_W = 64
N_MIXERS = 3
N_HEADS = 16
N_KV_HEADS = 4
HEAD_DIM = 64
GROUP = N_HEADS // N_KV_HEADS
ATTN_WIDTH = N_HEADS * HEAD_DIM
KV_WIDTH = N_KV_HEADS * HEAD_DIM
IN_WIDTH = 2 * ATTN_WIDTH + 2 * KV_WIDTH
Q_BLOCK = 128
WINDOW = 128
NA_MAX_ROWS = 8
NA_COLS = 16
ROPE_BASE = 10000.0
ROPE_PAIRS = HEAD_DIM // 4
EPS = 1e-6
NEG_INF = -1e30
N_B_LAYERS = (DEPTH + 1) // N_MIXERS
N_C_LAYERS = DEPTH // N_MIXERS

kernel_name = 'hybrid_dit_interleaved_attention_step'


def rms_norm(x, g):
    xf = x.astype(jnp.float32)
    y = xf * lax.rsqrt(jnp.mean(xf * xf, axis=-1, keepdims=True) + EPS)
    return (y * g.astype(jnp.float32)).astype(x.dtype)


def rope_tables(n):
    t = jnp.arange(n, dtype=jnp.int32)
    row = (t // GRID_W).astype(jnp.float32)
    col = (t % GRID_W).astype(jnp.float32)
    inv = ROPE_BASE ** (-jnp.arange(ROPE_PAIRS, dtype=jnp.float32) / ROPE_PAIRS)
    ang = jnp.concatenate([row[:, None] * inv, col[:, None] * inv], axis=-1)
    return jnp.cos(ang), jnp.sin(ang)


def apply_rope(x, cos, sin):
    half = HEAD_DIM // 2
    xf = x.astype(jnp.float32)
    x1, x2 = xf[..., :half], xf[..., half:]
    c_, s_ = cos[None, :, None, :], sin[None, :, None, :]
    return jnp.concatenate([x1 * c_ - x2 * s_, x2 * c_ + x1 * s_], axis=-1).astype(x.dtype)


def modulation(cond, w_mod_l, b_mod_l):
    m = jax.nn.silu(cond) @ w_mod_l + b_mod_l
    return jnp.split(m, 3, axis=-1)


def project(u, w_in_l, q_g, k_g):
    b_, t_ = u.shape[:2]
    p = u @ w_in_l
    q, k, v, z = jnp.split(p, [ATTN_WIDTH, ATTN_WIDTH + KV_WIDTH, ATTN_WIDTH + 2 * KV_WIDTH], axis=-1)
    q = rms_norm(q.reshape(b_, t_, N_HEADS, HEAD_DIM), q_g)
    k = rms_norm(k.reshape(b_, t_, N_KV_HEADS, HEAD_DIM), k_g)
    v = v.reshape(b_, t_, N_KV_HEADS, HEAD_DIM)
    return q, k, v, z


def softmax_with_sink(s, sink):
    if sink is None:
        return jax.nn.softmax(s, axis=-1)
    sk = sink.astype(jnp.float32)[None, :, :, None, None]
    m = jnp.maximum(jnp.max(s, axis=-1, keepdims=True), sk)
    p = jnp.exp(s - m)
    return p / (jnp.sum(p, axis=-1, keepdims=True) + jnp.exp(sk - m))


def to_blocks(q):
    b_, t_ = q.shape[:2]
    return q.reshape(b_, t_ // Q_BLOCK, Q_BLOCK, N_KV_HEADS, GROUP, HEAD_DIM).transpose(1, 0, 2, 3, 4, 5)


def from_blocks(o):
    nb, b_ = o.shape[:2]
    return o.transpose(1, 0, 2, 3, 4, 5).reshape(b_, nb * Q_BLOCK, ATTN_WIDTH)


def dense_attention(q, k, v, sink):
    scale = HEAD_DIM ** -0.5

    def block(qb):
        s = jnp.einsum('bqhgd,bkhd->bhgqk', qb * scale, k, preferred_element_type=jnp.float32)
        p = softmax_with_sink(s, sink).astype(v.dtype)
        return jnp.einsum('bhgqk,bkhd->bqhgd', p, v)

    return from_blocks(lax.map(block, to_blocks(q)))


def window_attention(q, k, v, k_ctx, v_ctx, sink):
    scale = HEAD_DIM ** -0.5
    n = q.shape[1]
    pad = ((0, 0), (WINDOW, WINDOW), (0, 0), (0, 0))
    kp, vp = jnp.pad(k, pad), jnp.pad(v, pad)
    band = Q_BLOCK + 2 * WINDOW
    kofs = jnp.arange(band) - WINDOW
    rel = kofs[None, :] - jnp.arange(Q_BLOCK)[:, None]

    def block(args):
        b, qb = args
        start = b * Q_BLOCK
        kb = lax.dynamic_slice_in_dim(kp, start, band, axis=1)
        vb = lax.dynamic_slice_in_dim(vp, start, band, axis=1)
        kpos = start + kofs
        valid = (jnp.abs(rel) <= WINDOW) & ((kpos >= 0) & (kpos < n))[None, :]
        qs = qb * scale
        s_loc = jnp.einsum('bqhgd,bkhd->bhgqk', qs, kb, preferred_element_type=jnp.float32)
        s_loc = jnp.where(valid, s_loc, NEG_INF)
        s_ctx = jnp.einsum('bqhgd,bkhd->bhgqk', qs, k_ctx, preferred_element_type=jnp.float32)
        p = softmax_with_sink(jnp.concatenate([s_loc, s_ctx], axis=-1), sink).astype(v.dtype)
        return (jnp.einsum('bhgqk,bkhd->bqhgd', p[..., :band], vb)
                + jnp.einsum('bhgqk,bkhd->bqhgd', p[..., band:], v_ctx))

    nb = n // Q_BLOCK
    return from_blocks(lax.map(block, (jnp.arange(nb), to_blocks(q))))


def neighborhood_attention(q, k, v, k_ctx, v_ctx, bias_table):
    scale = HEAD_DIM ** -0.5
    n = q.shape[1]
    rows = n // GRID_W
    wr = min(NA_MAX_ROWS, rows)
    wc = NA_COLS
    n_nb = wr * wc
    kr = jnp.arange(wr)
    kc = jnp.arange(wc)
    bias_h = bias_table.reshape(N_KV_HEADS, GROUP, 2 * NA_MAX_ROWS - 1, 2 * NA_COLS - 1)

    def block(args):
        b, qb = args
        t = b * Q_BLOCK + jnp.arange(Q_BLOCK)
        r, c_ = t // GRID_W, t % GRID_W
        rs = jnp.clip(r - wr // 2, 0, rows - wr)
        cs = jnp.clip(c_ - wc // 2, 0, GRID_W - wc)
        key_r = jnp.broadcast_to(rs[:, None, None] + kr[None, :, None], (Q_BLOCK, wr, wc))
        key_c = jnp.broadcast_to(cs[:, None, None] + kc[None, None, :], (Q_BLOCK, wr, wc))
        idx = (key_r * GRID_W + key_c).reshape(Q_BLOCK, n_nb)
        dr = (key_r - r[:, None, None] + NA_MAX_ROWS - 1).reshape(Q_BLOCK, n_nb)
        dc = (key_c - c_[:, None, None] + NA_COLS - 1).reshape(Q_BLOCK, n_nb)
        bias = bias_h[:, :, dr, dc].astype(jnp.float32)
        kb = jnp.take(k, idx, axis=1)
        vb = jnp.take(v, idx, axis=1)
        qs = qb * scale
        s_loc = jnp.einsum('bqhgd,bqkhd->bhgqk', qs, kb, preferred_element_type=jnp.float32) + bias[None]
        s_ctx = jnp.einsum('bqhgd,bkhd->bhgqk', qs, k_ctx, preferred_element_type=jnp.float32)
        p = jax.nn.softmax(jnp.concatenate([s_loc, s_ctx], axis=-1), axis=-1).astype(v.dtype)
        return (jnp.einsum('bhgqk,bqkhd->bqhgd', p[..., :n_nb], vb)
                + jnp.einsum('bhgqk,bkhd->bqhgd', p[..., n_nb:], v_ctx))

    nb = n // Q_BLOCK
    return from_blocks(lax.map(block, (jnp.arange(nb), to_blocks(q))))


def branch_out(o, z, w_out_l):
    return (o * jax.nn.silu(z)) @ w_out_l


def setup_inputs(seed: int = 0) -> dict:
    key = jax.random.key(seed)
    ks = jax.random.split(key, 17)
    f32 = jnp.float32
    nrm = lambda k_, shape: jax.random.normal(k_, shape, dtype=f32)
    return {
        'x_prompt': nrm(ks[0], (BATCH, SEQ, D_MODEL)),
        'x_sample': nrm(ks[1], (DEC_BATCH, DEC_SEQ, D_MODEL)),
        'cache_k': nrm(ks[2], (DEC_BATCH, DEPTH, PAST_LEN, N_KV_HEADS, HEAD_DIM)),
        'cache_v': nrm(ks[3], (DEC_BATCH, DEPTH, PAST_LEN, N_KV_HEADS, HEAD_DIM)),
        'c': nrm(ks[4], (DEC_BATCH, D_MODEL)),
        'c_ctx': nrm(ks[5], (D_MODEL,)),
        'w_mod': nrm(ks[6], (DEPTH, D_MODEL, 3 * D_MODEL)) * (0.5 * D_MODEL ** -0.5),
        'b_mod': nrm(ks[7], (DEPTH, 3 * D_MODEL)) * 0.02,
        'norm_pre': 1.0 + 0.05 * nrm(ks[8], (DEPTH, D_MODEL)),
        'norm_post': 1.0 + 0.05 * nrm(ks[9], (DEPTH, D_MODEL)),
        'w_in': nrm(ks[10], (DEPTH, D_MODEL, IN_WIDTH)) * D_MODEL ** -0.5,
        'q_norm': 1.0 + 0.05 * nrm(ks[11], (DEPTH, HEAD_DIM)),
        'k_norm': 1.0 + 0.05 * nrm(ks[12], (DEPTH, HEAD_DIM)),
        'w_out': nrm(ks[13], (DEPTH, ATTN_WIDTH, D_MODEL)) * ATTN_WIDTH ** -0.5,
        'sink_logit': 0.5 * nrm(ks[14], (N_B_LAYERS, N_HEADS)),
        'na_rel_bias': 0.1 * nrm(ks[15], (N_C_LAYERS, N_HEADS, 2 * NA_MAX_ROWS - 1, 2 * NA_COLS - 1)),
    }


def reference(x_prompt, x_sample, cache_k, cache_v, c, c_ctx, w_mod, b_mod, norm_pre, norm_post,
              w_in, q_norm, k_norm, w_out, sink_logit, na_rel_bias):
    n_lat = x_sample.shape[1]
    cos, sin = rope_tables(n_lat)
    h_ctx, h_lat = x_prompt, x_sample
    new_k, new_v = [], []
    for l in range(DEPTH):
        kind = l % N_MIXERS
        sink = sink_logit[l // N_MIXERS].reshape(N_KV_HEADS, GROUP) if kind == 1 else None

        shift, scl, gate = modulation(c_ctx, w_mod[l], b_mod[l])
        u = rms_norm(h_ctx, norm_pre[l]) * (1 + scl) + shift
        q, k, v, z = project(u, w_in[l], q_norm[l], k_norm[l])
        new_k.append(k)
        new_v.append(v)
        o = dense_attention(q, k, v, sink)
        h_ctx = h_ctx + gate * rms_norm(branch_out(o, z, w_out[l]), norm_post[l])

        shift, scl, gate = modulation(c, w_mod[l], b_mod[l])
        u = rms_norm(h_lat, norm_pre[l]) * (1 + scl[:, None, :]) + shift[:, None, :]
        q, k, v, z = project(u, w_in[l], q_norm[l], k_norm[l])
        k_ctx, v_ctx = cache_k[:, l], cache_v[:, l]
        if kind == 0:
            q, k = apply_rope(q, cos, sin), apply_rope(k, cos, sin)
            o = dense_attention(q, jnp.concatenate([k, k_ctx], axis=1),
                                jnp.concatenate([v, v_ctx], axis=1), None)
        elif kind == 1:
            q, k = apply_rope(q, cos, sin), apply_rope(k, cos, sin)
            o = window_attention(q, k, v, k_ctx, v_ctx, sink)
        else:
            o = neighborhood_attention(q, k, v, k_ctx, v_ctx, na_rel_bias[l // N_MIXERS])
        h_lat = h_lat + gate[:, None, :] * rms_norm(branch_out(o, z, w_out[l]), norm_post[l])

    new_cache_k = jnp.stack(new_k, axis=1)
    new_cache_v = jnp.stack(new_v, axis=1)
    return (h_ctx, h_lat, new_cache_k, new_cache_v)
```

```python
import numpy as np
from contextlib import ExitStack
import concourse.bass as bass
import concourse.mybir as mybir
from concourse.bass_utils import run_bass_kernel_spmd

F32 = mybir.dt.float32
BF16 = mybir.dt.bfloat16
AF = mybir.ActivationFunctionType
ALU = mybir.AluOpType
AX = mybir.AxisListType

D = 1024
NL = 4
NH = 16
NKV = 4
HD = 64
TL = 4096
TCX = 512
NQ = 256
NLT = TL // 128
NKB = NLT + 2
EPS = 1e-6
NEG = -30000.0
GRID_W = 64
PERM = [0, 4, 1, 5, 2, 6, 3, 7, 8, 12, 9, 13, 10, 14, 11, 15]


class Sem:
    def __init__(self, h, name):
        self.h = h
        self.cnt = 0
        self.name = name


class Buf:
    def __init__(self, name):
        self.name = name
        self.w = None
        self.r = {}


class Queue:
    def __init__(self, name, sem):
        self.name = name
        self.sem = sem
        self.ops = []
        self.seen = {}
        self.pending = False

    def wait(self, toks):
        for t in toks:
            if t is None:
                continue
            sem, val = t
            if sem is self.sem and (val > sem.cnt or self.name == "pe"):
                continue
            if self.seen.get(sem, 0) >= val:
                continue
            self.seen[sem] = val
            self.ops.append(("wait", sem, val))


def _round_robin(gens):
    gens = list(gens)
    while gens:
        for g in list(gens):
            try:
                yield next(g)
            except StopIteration:
                gens.remove(g)


def _mark(reads, writes, tok):
    for b in reads:
        if b.r.get(tok[0], 0) < tok[1]:
            b.r[tok[0]] = tok[1]
    for b in writes:
        b.w = tok
        b.r = {}


def _deps(reads, writes):
    deps = []
    for b in reads:
        deps.append(b.w)
    for b in writes:
        deps.append(b.w)
        deps.extend(b.r.items())
    return deps


class Prog:
    def __init__(self, n_layers=NL):
        self.n_layers = n_layers
        self.nc = bass.Bass("TRN2", target_bir_lowering=False)
        self.es = ExitStack()
        self.sems = []
        self.dma_sems = []

    def sem(self, name):
        s = Sem(self.es.enter_context(self.nc.semaphore(name)), name)
        self.sems.append(s)
        return s

    def dsem(self, name):
        s = self.sem(name)
        self.dma_sems.append(s)
        return s

    def sb(self, name, shape, dt):
        return self.es.enter_context(self.nc.sbuf_tensor(name, shape, dt))

    def psum(self, name, shape, dt):
        return self.es.enter_context(self.nc.psum_tensor(name, shape, dt))

    def run(self, q, fn, reads=(), writes=(), inc=True):
        q.wait(_deps(reads, writes))
        if inc:
            q.sem.cnt += 1
            tok = (q.sem, q.sem.cnt)
            q.ops.append(("op", fn, True))
            q.pending = False
        else:
            tok = (q.sem, q.sem.cnt + 1)
            q.ops.append(("op", fn, False))
            q.pending = True
        _mark(reads, writes, tok)
        return tok

    def dma(self, q, out, in_, sem, reads=(), writes=(), **kw):
        if isinstance(sem, str):
            if sem not in self.ds:
                self.ds[sem] = self.dsem("d_" + sem)
            sem = self.ds[sem]
        q.wait(_deps(reads, writes))
        sem.cnt += 16
        tok = (sem, sem.cnt)
        q.ops.append(("dma", out, in_, sem, kw))
        _mark(reads, writes, tok)
        return tok

    def barrier(self):
        toks = []
        for q in self.cq:
            assert not q.pending, q.name
            toks.append((q.sem, q.sem.cnt))
        for s in self.dma_sems:
            toks.append((s, s.cnt))
        for q in self.allq:
            q.wait(toks)

    def emit(self, q, e):
        for op in q.ops:
            if op[0] == "wait":
                e.wait_ge(op[1].h, op[2])
            elif op[0] == "op":
                ins = op[1](e)
                if op[2]:
                    ins.then_inc(q.sem.h, 1)
            else:
                e.dma_start(out=op[1], in_=op[2], **op[4]).then_inc(op[3].h, 16)

    def build(self):
        nc = self.nc
        dram = lambda n, s, k="ExternalInput", dt=F32: nc.dram_tensor(n, list(s), dt, kind=k)
        self.xs = dram("xs", (TL, D))
        self.xp = dram("xp", (TCX, D))
        self.ck = dram("ck", (NL, 256, 256))
        self.cv = dram("cv", (NL, 256, 256))
        self.cond = dram("cond", (2, D))
        self.w_mod = dram("w_mod", (NL, D, 3 * D))
        self.b_mod = dram("b_mod", (NL, 3 * D))
        self.norm_pre = dram("norm_pre", (NL, D))
        self.norm_post = dram("norm_post", (NL, D))
        self.w_in = dram("w_in", (NL, D, 2560))
        self.q_norm = dram("q_norm", (NL, HD))
        self.k_norm = dram("k_norm", (NL, HD))
        self.w_out = dram("w_out", (NL, D, D))
        self.sink = dram("sink", (1, NH))
        self.tsrc = dram("tsrc", (2, NH, 15, 128))
        self.cos_d = dram("cos_t", (128, NLT, 32))
        self.sin_d = dram("sin_t", (128, NLT, 32))
        self.ident_d = dram("ident_f", (128, 128))
        self.wmask_d = dram("wmask_f", (4, 128, NQ))
        self.cmask_d = dram("cmask_f", (2, 128, 15 * 64))
        self.ys = dram("ys", (TL, D), "ExternalOutput")
        self.yp = dram("yp", (TCX, D), "ExternalOutput")
        self.nk = dram("nk", (2, NL, 256, 256), "ExternalOutput")
        self.nv = dram("nv", (2, NL, 256, 256), "ExternalOutput")
        self.hs_d = dram("hs_scr", (TL, D), "Internal")
        self.hp_d = dram("hp_scr", (TCX, D), "Internal")
        self.modrow = dram("modrow", (NL, 2, 3 * D), "Internal")
        self.tzD = dram("tz_scr", (128, NH, 15 * 64), "Internal")
        self.nab = dram("nab_scr", (2, NH, 128, 15 * 64), "Internal", BF16)

        self.pe = Queue("pe", self.sem("s_pe"))
        self.act = Queue("act", self.sem("s_act"))
        self.dve = Queue("dve", self.sem("s_dve"))
        self.pool = Queue("pool", self.sem("s_pool"))
        self.sp = Queue("sp", self.sem("s_sp"))
        self.cq = [self.pe, self.act, self.dve, self.pool]
        self.allq = [self.pe, self.act, self.dve, self.pool, self.sp]

        sb, ps = self.sb, self.psum
        self.win = sb("win", [128, 8, 2560], BF16)
        self.wout = sb("wout", [128, 8, 1024], BF16)
        self.KT = sb("KT", [128, 2, NKB * 128], BF16)
        self.VV = sb("VV", [128, NKB, NKV, HD], BF16)
        self.KTc = self.KT
        self.VVc = self.VV
        self.mods = sb("mods", [128, 3 * D], F32)
        self.cosT = sb("cosT", [128, NLT, 32], F32)
        self.sinT = sb("sinT", [128, NLT, 32], F32)
        self.ident = sb("ident", [128, 128], BF16)
        self.ones64 = sb("ones64", [128, 64], BF16)
        self.A1 = sb("A1", [128, 1024], F32)
        self.A2 = sb("A2", [128, 1024], F32)
        self.A3 = sb("A3", [128, 1024], F32)
        self.A4 = sb("A4", [128, 1024], F32)
        self.B1 = sb("B1", [128, 1024], F32)
        self.B2 = sb("B2", [128, 1024], F32)
        self.hB = [sb("hB0", [128, 1024], F32), sb("hB1", [128, 1024], F32)]
        self.u = sb("u", [128, 1024], BF16)
        self.u1 = sb("u1", [128, 1024], BF16)
        self.P3 = sb("P3", [128, 1024], BF16)
        self.uTg = sb("uTg", [128, 8, NQ], BF16)
        self.QTs = [sb("QT0", [128, NH, NQ], BF16), sb("QT1", [128, NH, NQ], BF16)]
        self.zsTs = [sb("zsT0", [128, 8, NQ], BF16), sb("zsT1", [128, 8, NQ], BF16)]
        self.hR = [sb("hR0", [128, 1024], F32), sb("hR1", [128, 1024], F32)]
        self.R3 = sb("R3", [128, 1024], F32)
        self.R4 = sb("R4", [128, 1024], F32)
        self.S3 = sb("S3", [128, 1024], F32)
        self.S4 = sb("S4", [128, 1024], F32)
        self.kd = sb("kd", [128, NKV, HD], BF16)
        self.kd1 = sb("kd1", [128, NKV, HD], BF16)
        self.kv32 = self.R3[:, :].rearrange("p (i c) -> p i c", i=2)
        self.tbl = [sb("tbl0", [128, 1024], BF16), sb("tbl1", [128, 1024], BF16)]
        self.wmask = self.tbl[0][:, :].rearrange("p (o q) -> p o q", q=NQ)
        self.stat = sb("stat", [128, 256], F32)
        self.gq = sb("gq", [128, HD], F32)
        self.gk = sb("gk", [128, HD], F32)
        self.esink = sb("esink", [128, NH], F32)
        self.mhalf = sb("mhalf", [128, 16], F32)
        self.condT = sb("condT", [128, 2, 8], F32)
        self.scT = sb("scT", [128, 8, 2], F32)
        self.ps_s = [ps("ps_s0", [128, 1024], F32), ps("ps_s1", [128, 1024], F32)]
        self.ps_oo = [ps("ps_o0", [128, 512], F32), ps("ps_o1", [128, 512], F32)]
        self.ps_p = ps("ps_p", [128, 1024], F32)
        self.ps_t = self.ps_p[:, 512:1024].bitcast(BF16)

        B = Buf
        self.b = {n: B(n) for n in [
            "win_kv", "win_q", "win_z", "wout", "mods", "consts", "A1", "A2", "A3a", "A3b", "A4a", "A4b", "A4c", "A4d",
            "hB0", "hB1", "hR0", "hR1", "R3a", "R3b", "R4a", "R4b", "S3a", "S3b", "S4a", "S4b", "B1", "B2", "u", "u1", "kd1", "P3", "uTg0", "uTg1", "pb0", "pb1", "pb2", "pb3", "st_ln1", "st_q1", "st_y1", "QT0", "QT1", "zsT0", "zsT1", "kd", "tbl0", "tbl1", "gq", "gk",
            "esink", "condT", "scT", "ps_s0", "ps_s1", "ps_o0", "ps_o1", "ps_t", "ps_p0", "ps_p1",
            "st_ln", "st_q", "st_k", "st_y", "modrow", "tzD", "nab"]}
        self.b["ps_t"] = self.b["ps_p1"]
        self.mk_scratch_sets()
        self.b["kv32a"] = self.b["R3a"]
        self.b["kv32b"] = self.b["R3b"]
        self.bKT = [B("KT%d" % i) for i in range(NKB)]
        self.bVV = [B("VV%d" % i) for i in range(NKB)]
        self.bKTc = self.bKT
        self.bVVc = self.bVV
        self.bhs = [B("hs%d" % i) for i in range(NLT)]
        self.bhp = [B("hp%d" % i) for i in range(4)]
        self.bout = B("outkv")
        self.ds = {}

        self.prologue()
        for l in range(self.n_layers):
            self.layer(l)
        self.barrier()

        blk = self.es.enter_context(nc.Block())

        @blk.tensor
        def _(e):
            self.emit(self.pe, e)

        @blk.scalar
        def _(e):
            self.emit(self.act, e)

        @blk.vector
        def _(e):
            self.emit(self.dve, e)

        @blk.gpsimd
        def _(e):
            self.emit(self.pool, e)

        @blk.sync
        def _(e):
            self.emit(self.sp, e)

        self.es.close()
        return nc

    def mk_scratch_sets(self):
        b = self.b

        class SC:
            pass
        self.sc = []
        for k in range(2):
            sc = SC()
            sc.k = k
            sc.A1, sc.bA1 = (self.A1, b["A1"]) if k == 0 else (self.B1, b["B1"])
            sc.A2, sc.bA2 = (self.A2, b["A2"]) if k == 0 else (self.B2, b["B2"])
            sc.kA2 = "A2" if k == 0 else "B2"
            sc.R3, sc.bR3a, sc.bR3b = (self.R3, b["R3a"], b["R3b"]) if k == 0 else (self.S3, b["S3a"], b["S3b"])
            sc.kR3 = "R3a" if k == 0 else "S3a"
            sc.R4, sc.bR4a, sc.bR4b = (self.R4, b["R4a"], b["R4b"]) if k == 0 else (self.S4, b["S4a"], b["S4b"])
            sc.u, sc.bu = (self.u, b["u"]) if k == 0 else (self.u1, b["u1"])
            sc.kd, sc.bkd = (self.kd, b["kd"]) if k == 0 else (self.kd1, b["kd1"])
            sc.hB, sc.bhB, sc.khB = self.hB[k], b["hB%d" % k], "hB%d" % k
            sc.hR, sc.bhR, sc.khR = self.hR[k], b["hR%d" % k], "hR%d" % k
            sc.uT, sc.buT = self.uTg[:, :, k * 128:(k + 1) * 128], b["uTg%d" % k]
            sc.so = 128 * k
            sc.bst_ln = b["st_ln"] if k == 0 else b["st_ln1"]
            sc.bst_q = b["st_q"] if k == 0 else b["st_q1"]
            sc.bst_y = b["st_y"] if k == 0 else b["st_y1"]
            self.sc.append(sc)
        self.use_psum(0, own=True)
        self.use_psum(1, own=True)

    def use_psum(self, k, own):
        b = self.b
        sc = self.sc[k]
        if k == 0 or not own:
            sc.ps_p, sc.bps = self.ps_p, [b["ps_p0"], b["ps_p1"]]
        else:
            sc.ps_p, sc.bps = self.ps_s[0], [b["pb0"], b["pb1"]]
        sc.ps_t, sc.bpt = sc.ps_p[:, 512:1024].bitcast(BF16), sc.bps[1]

    def prologue(self):
        b, ds = self.b, self.ds
        c = [b["consts"]]
        self.dma(self.sp, self.cosT[:], self.cos_d.ap()[:, :, :], "consts", writes=c)
        self.dma(self.sp, self.sinT[:], self.sin_d.ap()[:, :, :], "consts", writes=c)
        self.dma(self.pool, self.ident[:], self.ident_d.ap()[:, :], "sw_consts", writes=c)
        self.run(self.pool, lambda e: e.memset(self.ones64[:], 1.0), writes=c)
        self.run(self.pool, lambda e: e.memset(self.mhalf[:], -0.5), writes=c)
        for i_ in range(2):
            self.run(self.pool, lambda e, i_=i_: e.memset(self.QTs[i_][:], 0.0), writes=[b["QT%d" % i_]])
        self.dma(self.sp, self.esink[:], self.sink.ap()[0:1, :].partition_broadcast(128), "esink", writes=[b["esink"]])
        self.run(self.act, lambda e: e.activation(out=self.esink[:], in_=self.esink[:], func=AF.Exp),
                 reads=[b["esink"]], writes=[b["esink"]])
        for j in range(2):
            self.dma(self.sp, self.condT[:, j, :], bass.AP(self.cond, j * D, [[1, 128], [128, 8]]), "condT",
                     writes=[b["condT"]], allow_slow_non_contiguous=True)
        self.run(self.act, lambda e: e.activation(out=self.scT[:, :, :].rearrange("p k j -> p j k"), in_=self.condT[:, :, :], func=AF.Silu),
                 reads=[b["condT"]], writes=[b["scT"]])
        if self.n_layers > 2:
            self.na_tables()
        self.barrier()

    def na_tables(self):
        b, ds = self.b, self.ds
        for half in range(2):
            for kc in range(64):
                p = half * 64 + kc
                self.dma(self.sp, self.tzD.ap()[p:p + 1, :, :].rearrange("p h (s c) -> p (h s) c", c=64),
                         bass.AP(self.tsrc, half * NH * 15 * 128 + 63 - kc, [[0, 1], [128, NH * 15], [1, 64]]),
                         "tzD", writes=[b["tzD"]])
        cm = [self.A2, self.A3]
        self.dma(self.sp, self.A2[:, 0:960], self.cmask_d.ap()[0], "A2", writes=[b["A2"]])
        self.dma(self.sp, self.A3[:, 0:960], self.cmask_d.ap()[1], "A3a", writes=[b["A3a"], b["A3b"]])
        cmb = [[b["A2"]], [b["A3a"], b["A3b"]]]
        ob = self.A4[:, 0:480].bitcast(BF16)
        for h in range(NH):
            self.dma(self.sp, self.A1[:, 0:960], self.tzD.ap()[:, h, :], "A1", reads=[b["tzD"]], writes=[b["A1"]])
            for v in range(2):
                self.run(self.dve, lambda e, v=v: e.tensor_tensor(out=ob, in0=self.A1[:, 0:960], in1=cm[v][:, 0:960], op=ALU.add),
                         reads=[b["A1"]] + cmb[v], writes=[b["A4a"], b["A4b"]])
                self.dma(self.sp, self.nab.ap()[v, h], ob, "A4a", reads=[b["A4a"], b["A4b"]], writes=[b["nab"]])

    def layer(self, l):
        b, ds = self.b, self.ds
        kind = l % 3
        for (c0, c1, key) in ((1024, 1536, "win_kv"), (0, 1024, "win_q"), (1536, 2560, "win_z")):
            for kc in range(8):
                self.dma(self.pool, self.win[:, kc, c0:c1], self.w_in.ap()[l, kc * 128:(kc + 1) * 128, c0:c1], key, writes=[b[key]])
        for kc in range(8):
            self.dma(self.pool, self.wout[:, kc, :], self.w_out.ap()[l, kc * 128:(kc + 1) * 128, :], "wout", writes=[b["wout"]])
        self.dma(self.sp, self.gq[:], self.q_norm.ap()[l:l + 1, :].partition_broadcast(128), "gq", writes=[b["gq"]])
        self.dma(self.sp, self.gk[:], self.k_norm.ap()[l:l + 1, :].partition_broadcast(128), "gk", writes=[b["gk"]])
        self.run(self.dve, lambda e: e.tensor_scalar(self.gq[:], self.gq[:], HD ** -0.5, None, ALU.mult),
                 reads=[b["gq"]], writes=[b["gq"]])
        import os
        stg = int(os.environ.get("DBG_STAGE", "99"))
        nch = int(os.environ.get("DBG_NCH", str(TL // NQ)))
        if stg < 2:
            return
        if kind == 1:
            self.dma(self.pool, self.wmask, self.wmask_d.ap().rearrange("o p q -> p o q"), "sw_tbl0", writes=[b["tbl0"]])
        self.modulation_rows(l)
        if stg < 3:
            return
        self.load_mods(l, 1)
        self.phase_a(l, 4, ctx=True)
        if stg < 4:
            return
        self.phase_b(l, 2, ctx=True)
        if stg < 5:
            return
        self.load_mods(l, 0)
        self.cached_kv(l)
        self.phase_a(l, NLT, ctx=False)
        if stg < 6:
            return
        self.phase_b(l, nch, ctx=False)

    def modulation_rows(self, l):
        b, ds = self.b, self.ds
        stg = [self.A1, self.A2, self.hB[0], self.hB[1]]
        wm = [t_[:, :].rearrange("p (k n) -> p k n", n=128) for t_ in stg]
        wmb = [b["A1"], b["A2"], b["hB0"], b["hB1"]]
        wmk = ["A1", "A2", "hB0", "hB1"]
        NCH = 3 * D // 128
        for ch in range(NCH):
            i = ch % 4
            self.dma(self.sp, wm[i], self.w_mod.ap()[l, :, ch * 128:(ch + 1) * 128].rearrange("(k p) n -> p k n", p=128),
                     wmk[i], writes=[wmb[i]])
            pv = self.ps_p[0:2, (ch % 4) * 128:(ch % 4) * 128 + 128]
            pb = b["ps_p0"]
            for kc in range(8):
                self.run(self.pe, lambda e, kc=kc, i=i, pv=pv: e.matmul(pv, lhsT=self.scT[:, kc, :], rhs=wm[i][:, kc, :],
                                                                   start=(kc == 0), stop=(kc == 7)),
                         reads=[wmb[i], b["scT"]], writes=[pb], inc=(kc == 7))
            if ch % 4 == 3:
                c0 = (ch - 3) * 128
                bm = self.A3[0:2, 0:512]
                mo = self.A3[0:2, 512:1024]
                self.dma(self.sp, bm, self.b_mod.ap()[l:l + 1, c0:c0 + 512].partition_broadcast(2), "A3a", writes=[b["A3a"]])
                self.run(self.dve, lambda e, bm=bm, mo=mo: e.tensor_tensor(out=mo, in0=self.ps_p[0:2, 0:512], in1=bm, op=ALU.add),
                         reads=[pb, b["A3a"]], writes=[b["A3b"]])
                self.dma(self.sp, self.modrow.ap()[l, :, c0:c0 + 512], mo, "A3b", reads=[b["A3b"]], writes=[b["modrow"]])

    def load_mods(self, l, j):
        b, ds = self.b, self.ds
        self.dma(self.sp, self.mods[:], self.modrow.ap()[l, j:j + 1, :].partition_broadcast(128), "mods",
                 reads=[b["modrow"]], writes=[b["mods"]])
        self.dma(self.sp, self.A1[:], self.norm_pre.ap()[l:l + 1, :].partition_broadcast(128), "A1", writes=[b["A1"]])
        self.dma(self.sp, self.A2[:], self.norm_post.ap()[l:l + 1, :].partition_broadcast(128), "A2", writes=[b["A2"]])
        self.run(self.dve, lambda e: e.scalar_tensor_tensor(out=self.mods[:, D:2 * D], in0=self.mods[:, D:2 * D], scalar=1.0,
                                                            in1=self.A1[:], op0=ALU.add, op1=ALU.mult),
                 reads=[b["mods"], b["A1"]], writes=[b["mods"]])
        self.run(self.dve, lambda e: e.tensor_tensor(out=self.mods[:, 2 * D:3 * D], in0=self.mods[:, 2 * D:3 * D], in1=self.A2[:], op=ALU.mult),
                 reads=[b["mods"], b["A2"]], writes=[b["mods"]])

    def rstd_small(self, ss, tmp, out, n, inv_n, bufs):
        self.run(self.pool, lambda e: e.tensor_scalar(tmp, ss, inv_n, EPS, ALU.mult, ALU.add), reads=bufs, writes=bufs)
        self.run(self.pool, lambda e: e.tensor_tensor(out=out, in0=tmp, in1=self.mhalf[:, 0:n], op=ALU.pow),
                 reads=bufs + [self.b["consts"]], writes=bufs)

    def ln_u_g(self, sc, hbuf, hB):
        ps_p, ps_t, bps, bpt = sc.ps_p, sc.ps_t, sc.bps, sc.bpt
        b = self.b
        st = self.stat
        o = sc.so
        sb_ = [sc.bst_ln]
        self.run(self.act, lambda e: e.activation(out=sc.A1[:], in_=hbuf[:], func=AF.Square, accum_out=st[:, o:o + 1]),
                 reads=[hB], writes=[sc.bA1, sc.bst_ln])
        yield
        self.rstd_small(st[:, o:o + 1], st[:, o + 1:o + 2], st[:, o + 2:o + 3], 1, 1.0 / D, sb_)
        yield
        self.run(self.dve, lambda e: e.scalar_tensor_tensor(out=sc.A2[:], in0=hbuf[:], scalar=st[:, o + 2:o + 3], in1=self.mods[:, D:2 * D],
                                                            op0=ALU.mult, op1=ALU.mult),
                 reads=[hB, sc.bst_ln, b["mods"]], writes=[sc.bA2])
        yield
        self.run(self.dve, lambda e: e.tensor_tensor(out=sc.u[:], in0=sc.A2[:], in1=self.mods[:, 0:D], op=ALU.add),
                 reads=[sc.bA2, b["mods"]], writes=[sc.bu])
        yield
        for kc in range(8):
            self.run(self.pe, lambda e, kc=kc: e.transpose(ps_t[:, kc * 128:(kc + 1) * 128], sc.u[:, kc * 128:(kc + 1) * 128], self.ident[:]),
                     reads=[sc.bu, b["consts"]], writes=[bpt], inc=(kc == 7))
        yield
        self.run(self.dve, lambda e: e.tensor_copy(out=sc.uT, in_=ps_t[:, :].rearrange("p (k t) -> p k t", t=128)),
                 reads=[bpt], writes=[sc.buT])
        yield

    def prep_heads_g(self, sc, src, nh, g, rope_tile, sbuf, outs):
        b = self.b
        RT, rba, rbb = sc.R3, sc.bR3a, sc.bR3b
        W = nh * HD
        st = self.stat
        o = sc.so
        if nh == NH:
            ss, ln, rs = st[:, o + 8:o + 24], st[:, o + 24:o + 40], st[:, o + 40:o + 56]
        else:
            ss, ln, rs = st[:, o + 60:o + 64], st[:, o + 64:o + 68], st[:, o + 68:o + 72]
        sq = sc.A1[:, 0:W]
        xn = sc.A2[:, 0:W]
        v3 = lambda ap: ap.rearrange("p (h d) -> p h d", d=HD)
        self.run(self.act, lambda e: e.activation(out=sq, in_=src, func=AF.Square), reads=sbuf, writes=[sc.bA1])
        yield
        self.run(self.dve, lambda e: e.tensor_reduce(out=ss, in_=v3(sq), axis=AX.X, op=ALU.add), reads=[sc.bA1], writes=[sc.bst_q])
        yield
        self.rstd_small(ss, ln, rs, nh, 1.0 / HD, [sc.bst_q])
        yield
        self.run(self.dve, lambda e: e.tensor_tensor(out=v3(xn), in0=v3(src), in1=g[:, :].unsqueeze(1).to_broadcast([128, nh, HD]), op=ALU.mult),
                 reads=sbuf + [b["gq"], b["gk"]], writes=[sc.bA2])
        yield
        fin, finb = xn, sc.bA2
        if rope_tile is not None:
            cosb = self.cosT[:, rope_tile, :].unsqueeze(1).to_broadcast([128, nh, 32])
            sinb = self.sinT[:, rope_tile, :].unsqueeze(1).to_broadcast([128, nh, 32])
            H = nh * 32
            ra = RT[:, 0:H].rearrange("p (h d) -> p h d", d=32)
            rb = RT[:, 512:512 + H].rearrange("p (h d) -> p h d", d=32)
            x1, x2 = v3(xn)[:, :, 0:32], v3(xn)[:, :, 32:64]
            xr = sc.A1[:, 0:W]
            o1, o2 = v3(xr)[:, :, 0:32], v3(xr)[:, :, 32:64]
            tt = lambda out, i0, i1, op, rd, wr: self.run(
                self.dve, lambda e: e.tensor_tensor(out=out, in0=i0, in1=i1, op=op), reads=rd, writes=wr)
            cb = [b["consts"]]
            tt(ra, x1, cosb, ALU.mult, [sc.bA2] + cb, [rba])
            tt(rb, x2, sinb, ALU.mult, [sc.bA2] + cb, [rbb])
            yield
            tt(o1, ra, rb, ALU.subtract, [rba, rbb], [sc.bA1])
            yield
            tt(ra, x2, cosb, ALU.mult, [sc.bA2] + cb, [rba])
            tt(rb, x1, sinb, ALU.mult, [sc.bA2] + cb, [rbb])
            yield
            tt(o2, ra, rb, ALU.add, [rba, rbb], [sc.bA1])
            yield
            fin, finb = xr, sc.bA1
        for (oap, wb) in outs:
            i0 = v3(fin)
            i1 = rs.unsqueeze(2).to_broadcast([128, nh, HD])
            self.run(self.dve, lambda e, oap=oap, i0=i0, i1=i1: e.tensor_tensor(out=oap, in0=i0, in1=i1, op=ALU.mult),
                     reads=[finb, sc.bst_q], writes=wb)
            yield

    def kt_store(self, sc, kdst, kbuf):
        ps_p, ps_t, bps, bpt = sc.ps_p, sc.ps_t, sc.bps, sc.bpt
        b = self.b
        for kp in range(2):
            self.run(self.pe, lambda e, kp=kp: e.transpose(ps_t[:, kp * 128:(kp + 1) * 128],
                                                      sc.kd[:, 2 * kp:2 * kp + 2, :].rearrange("p a d -> p (a d)"), self.ident[:]),
                     reads=[sc.bkd, b["consts"]], writes=[bpt], inc=(kp == 1))
        self.run(self.dve, lambda e: e.tensor_copy(out=kdst, in_=ps_t[:, 0:256].rearrange("p (k t) -> p k t", t=128)),
                 reads=[bpt], writes=[kbuf])

    def cached_kv(self, l):
        b = self.b
        for j in range(2):
            sc = self.sc[j]
            ps_p, ps_t, bps, bpt = sc.ps_p, sc.ps_t, sc.bps, sc.bpt
            kb = NLT + j
            hb_, hbuf = sc.bhB, sc.hB
            self.dma(self.sp, hbuf[:, 0:256], self.ck.ap()[l, j * 128:(j + 1) * 128, :], sc.khB, writes=[hb_])
            self.dma(self.sp, hbuf[:, 256:512], self.cv.ap()[l, j * 128:(j + 1) * 128, :], sc.khB, writes=[hb_])
            self.run(self.dve, lambda e, hbuf=hbuf, sc=sc: e.tensor_copy(out=sc.kd[:, :, :].rearrange("p h d -> p (h d)"), in_=hbuf[:, 0:256]),
                     reads=[hb_], writes=[sc.bkd])
            self.kt_store(sc, self.KT[:, :, kb * 128:(kb + 1) * 128], self.bKT[kb])
            self.run(self.act, lambda e, kb=kb, hbuf=hbuf: e.activation(out=self.VV[:, kb, :, :].rearrange("p h d -> p (h d)"),
                                                                       in_=hbuf[:, 256:512], func=AF.Copy),
                     reads=[hb_], writes=[self.bVV[kb]])

    def phase_a_tile_g(self, l, t, ctx, sc):
        ps_p, ps_t, bps, bpt = sc.ps_p, sc.ps_t, sc.bps, sc.bpt
        b = self.b
        kind = l % 3
        hbuf, hB = sc.hB, sc.bhB
        src, hd = self._hsrc(l, t, ctx)
        self.dma(self.sp, hbuf[:], src, sc.khB, reads=[hd], writes=[hB])
        yield
        yield from self.ln_u_g(sc, hbuf, hB)
        pk = ps_p[:, 0:512]
        for kc in range(8):
            self.run(self.pe, lambda e, kc=kc: e.matmul(pk, lhsT=sc.uT[:, kc, :], rhs=self.win[:, kc, 1024:1536],
                                                   start=(kc == 0), stop=(kc == 7)),
                     reads=[sc.buT, b["win_kv"]], writes=[bps[0]], inc=(kc == 7))
        yield
        rope = (not ctx) and kind != 2
        outs = [(sc.kd[:], [sc.bkd])]
        kv32 = sc.R3[:, 0:512]
        if ctx:
            outs.append((kv32[:, 0:256].rearrange("p (h d) -> p h d", d=HD), [sc.bR3a]))
        yield from self.prep_heads_g(sc, ps_p[:, 0:256], NKV, self.gk, t if rope else None, [bps[0]], outs)
        self.kt_store(sc, self.KT[:, :, t * 128:(t + 1) * 128], self.bKT[t])
        yield
        self.run(self.act, lambda e: e.activation(out=self.VV[:, t, :, :].rearrange("p h d -> p (h d)"), in_=ps_p[:, 256:512], func=AF.Copy),
                 reads=[bps[0]], writes=[self.bVV[t]])
        yield
        if ctx:
            self.run(self.act, lambda e: e.activation(out=kv32[:, 256:512], in_=ps_p[:, 256:512], func=AF.Copy),
                     reads=[bps[0]], writes=[sc.bR3a])
            s_, r0 = t // 2, (t % 2) * 128
            self.dma(self.sp, self.nk.ap()[s_, l, r0:r0 + 128, :], kv32[:, 0:256], sc.kR3, reads=[sc.bR3a], writes=[self.bout])
            self.dma(self.sp, self.nv.ap()[s_, l, r0:r0 + 128, :], kv32[:, 256:512], sc.kR3, reads=[sc.bR3a], writes=[self.bout])
            yield

    def phase_a(self, l, ntiles, ctx):
        self.use_psum(1, own=True)
        for t in range(0, ntiles, 2):
            gens = [self.phase_a_tile_g(l, t, ctx, self.sc[0]), self.phase_a_tile_g(l, t + 1, ctx, self.sc[1])]
            for _ in _round_robin(gens):
                pass

    def _hsrc(self, l, t, ctx):
        if ctx:
            return (self.xp if l == 0 else self.hp_d).ap()[t * 128:(t + 1) * 128, :], self.bhp[t]
        return (self.xs if l == 0 else self.hs_d).ap()[t * 128:(t + 1) * 128, :], self.bhs[t]

    def stage1_tile_g(self, l, c, j, ctx, sc):
        ps_p, ps_t, bps, bpt = sc.ps_p, sc.ps_t, sc.bps, sc.bpt
        b = self.b
        kind = l % 3
        si = c % 2
        QT, QTb = self.QTs[si], b["QT%d" % si]
        zsT, zsTb = self.zsTs[si], b["zsT%d" % si]
        ppb = bps
        t = 2 * c + j
        hbuf, hB = sc.hB, sc.bhB
        src, hd = self._hsrc(l, t, ctx)
        self.dma(self.sp, hbuf[:], src, sc.khB, reads=[hd], writes=[hB])
        yield
        yield from self.ln_u_g(sc, hbuf, hB)
        for (c0, isq) in ((0, True), (1536, False)):
            for n in range(2):
                for kc in range(8):
                    self.run(self.pe, lambda e, kc=kc, n=n, c0=c0: e.matmul(
                        ps_p[:, n * 512:(n + 1) * 512], lhsT=sc.uT[:, kc, :],
                        rhs=self.win[:, kc, c0 + n * 512:c0 + (n + 1) * 512], start=(kc == 0), stop=(kc == 7)),
                        reads=[sc.buT, b["win_q"] if isq else b["win_z"]], writes=[ppb[n]], inc=(kc == 7))
                yield
            if isq:
                rope = (not ctx) and kind != 2
                qr = sc.R4[:, 0:512].bitcast(BF16)
                srcb = [sc.bR4a]
                yield from self.prep_heads_g(sc, ps_p[:, :], NH, self.gq, t if rope else None, ppb,
                                             [(qr.rearrange("p (h d) -> p h d", d=HD), srcb)])
                srcT = qr
            else:
                zs = sc.R4[:, 512:1024].bitcast(BF16)
                srcb = [sc.bR4b]
                self.run(self.act, lambda e: e.activation(out=sc.A1[:], in_=ps_p[:, :], func=AF.Tanh, scale=0.5),
                         reads=ppb, writes=[sc.bA1])
                yield
                self.run(self.dve, lambda e, zs=zs: e.scalar_tensor_tensor(out=zs, in0=sc.A1[:], scalar=1.0, in1=ps_p[:, :],
                                                                      op0=ALU.add, op1=ALU.mult),
                         reads=ppb + [sc.bA1], writes=srcb)
                yield
                srcT = zs
            for pr in range(8):
                self.run(self.pe, lambda e, pr=pr, srcT=srcT: e.transpose(ps_t[:, pr * 128:(pr + 1) * 128],
                                                                     srcT[:, pr * 128:(pr + 1) * 128], self.ident[:]),
                         reads=srcb + [b["consts"]], writes=[bpt], inc=(pr == 7))
            yield
            if isq:
                for par in range(2):
                    qo = QT[par * 64:(par + 1) * 64, :, :].rearrange("p (pr two) q -> p pr two q", two=2)[:, :, par, j * 128:(j + 1) * 128]
                    self.run(self.dve, lambda e, qo=qo, par=par: e.tensor_copy(
                        out=qo, in_=ps_t[par * 64:(par + 1) * 64, :].rearrange("p (k t) -> p k t", t=128)),
                        reads=[bpt], writes=[QTb])
            else:
                self.run(self.dve, lambda e: e.tensor_copy(out=zsT[:, :, j * 128:(j + 1) * 128],
                                                          in_=ps_t[:, :].rearrange("p (k t) -> p k t", t=128)),
                         reads=[bpt], writes=[zsTb])
            yield

    def stage3_tile_g(self, l, c, j, ctx, sc):
        ps_p, ps_t, bps, bpt = sc.ps_p, sc.ps_t, sc.bps, sc.bpt
        b = self.b
        si = c % 2
        gT, gTb = self.zsTs[si], b["zsT%d" % si]
        last = (l == self.n_layers - 1)
        ppb = bps
        st = self.stat
        o = sc.so
        t = 2 * c + j
        hbuf, hB = sc.hR, sc.bhR
        src, hd = self._hsrc(l, t, ctx)
        self.dma(self.sp, hbuf[:], src, sc.khR, reads=[hd], writes=[hB])
        yield
        for n in range(2):
            for pr in range(8):
                self.run(self.pe, lambda e, pr=pr, n=n: e.matmul(
                    ps_p[:, n * 512:(n + 1) * 512], lhsT=gT[:, pr, j * 128:(j + 1) * 128],
                    rhs=self.wout[:, pr, n * 512:(n + 1) * 512], start=(pr == 0), stop=(pr == 7)),
                    reads=[gTb, b["wout"]], writes=[ppb[n]], inc=(pr == 7))
            yield
        self.run(self.act, lambda e: e.activation(out=sc.A1[:], in_=ps_p[:, :], func=AF.Square, accum_out=st[:, o + 80:o + 81]),
                 reads=ppb, writes=[sc.bA1, sc.bst_y])
        yield
        self.rstd_small(st[:, o + 80:o + 81], st[:, o + 81:o + 82], st[:, o + 82:o + 83], 1, 1.0 / D, [sc.bst_y])
        yield
        self.run(self.dve, lambda e: e.scalar_tensor_tensor(out=sc.A1[:], in0=ps_p[:, :], scalar=st[:, o + 82:o + 83],
                                                            in1=self.mods[:, 2 * D:3 * D], op0=ALU.mult, op1=ALU.mult),
                 reads=ppb + [sc.bst_y, b["mods"]], writes=[sc.bA1])
        yield
        self.run(self.dve, lambda e: e.tensor_tensor(out=sc.A2[:], in0=sc.A1[:], in1=hbuf[:], op=ALU.add),
                 reads=[sc.bA1, hB], writes=[sc.bA2])
        yield
        if ctx:
            dst = (self.yp if last else self.hp_d).ap()[t * 128:(t + 1) * 128, :]
            hd = self.bhp[t]
        else:
            dst = (self.ys if last else self.hs_d).ap()[t * 128:(t + 1) * 128, :]
            hd = self.bhs[t]
        self.dma(self.sp, dst, sc.A2[:], sc.kA2, reads=[sc.bA2], writes=[hd])
        yield

    def phase_b(self, l, nch, ctx):
        import itertools
        kind = l % 3
        dual = (kind != 0)

        def fill(c3, c1):
            chains = []
            for j in range(2):
                g = []
                if c3 is not None:
                    g.append(self.stage3_tile_g(l, c3, j, ctx, self.sc[j]))
                if c1 is not None:
                    g.append(self.stage1_tile_g(l, c1, j, ctx, self.sc[j]))
                chains.append(itertools.chain(*g))
            return _round_robin(chains) if dual else itertools.chain(*chains)

        self.use_psum(1, own=True)
        for _ in _round_robin([self.stage1_tile_g(l, 0, j, ctx, self.sc[j]) for j in range(2)]):
            pass
        self.use_psum(1, own=dual)
        for c in range(nch):
            filler = fill(c - 1 if c > 0 else None, c + 1 if c + 1 < nch else None)
            self.attention(l, c, ctx, filler)
            for _ in filler:
                pass
        self.use_psum(1, own=True)
        for _ in _round_robin([self.stage3_tile_g(l, nch - 1, j, ctx, self.sc[j]) for j in range(2)]):
            pass

    def attention(self, l, c, ctx, filler=None):
        b, ds = self.b, self.ds
        si = c % 2
        QT, QTb = self.QTs[si], b["QT%d" % si]
        zsT, zsTb = self.zsTs[si], b["zsT%d" % si]
        kind = l % 3
        sink = (kind == 1)
        blocks = []
        if ctx:
            for j in range(2):
                kb = 2 * c + j
                blocks.append(("c", kb, None))
        else:
            if kind == 0:
                for kb in range(NLT):
                    blocks.append(("l", kb, None))
            elif kind == 1:
                for o in range(-1, 3):
                    kb = 2 * c + o
                    if 0 <= kb < NLT:
                        blocks.append(("l", kb, ("w", o + 1)))
            else:
                for o in range(-2, 4):
                    kb = 2 * c + o
                    if 0 <= kb < NLT:
                        blocks.append(("l", kb, ("n", 7 - 2 * o)))
            blocks.append(("l", NLT, None))
            blocks.append(("l", NLT + 1, None))
        import os
        if not ctx:
            blocks = blocks[:int(os.environ.get('DBG_NKB', '99'))]
        na = (not ctx) and kind == 2
        variant = 1 if (c == 0 or c == TL // NQ - 1) else 0
        if kind == 0:
            GS = 4
            Sap = [self.ps_s[0][:, :], self.ps_s[1][:, :]]
            psb = [[b["pb0"], b["pb1"]], [b["pb2"], b["pb3"]]]
        else:
            GS = 2
            Sap = [self.ps_s[1][:, 0:512], self.ps_s[1][:, 512:1024]]
            psb = [[b["pb2"]], [b["pb3"]]]
        groups = [blocks[i:i + GS] for i in range(0, len(blocks), GS)]
        units = []
        for h in range(NH if ctx else int(os.environ.get('DBG_NHD', '16'))):
            for gi, g in enumerate(groups):
                units.append((h, gi, g, gi == 0, gi == len(groups) - 1))
        P = [self.A3[:, 0:512].bitcast(BF16), self.A3[:, 512:1024].bitcast(BF16), self.P3[:, :]]
        Pb = [b["A3a"], b["A3b"], b["P3"]]
        ob = [b["ps_o0"], b["ps_o1"]]
        state = {"acc_started": {}}

        def kt_ap(kindb, kb, kv, base):
            if kindb == "c":
                return self.KTc[:, kv // 2, kb * 128:(kb + 1) * 128], self.bKTc[kb]
            return self.KT[:, kv // 2, kb * 128:(kb + 1) * 128], self.bKT[kb]

        def v_ap(kindb, kb, kv):
            if kindb == "c":
                return self.VVc[:, kb, kv, :], self.bVVc[kb]
            return self.VV[:, kb, kv, :], self.bVV[kb]

        def emit_qk(ui):
            h, gi, g, first, lastg = units[ui]
            kv, base = PERM[h] // 4, (h % 2) * 64
            S, Sb = Sap[ui % 2], psb[ui % 2]
            if na and gi == 0:
                ti = h % 2
                self.dma(self.sp, self.tbl[ti][:, 0:960], self.nab.ap()[variant, PERM[h]], "tbl%d" % ti, reads=[b["nab"]], writes=[b["tbl%d" % ti]])
            for j, (kb_kind, kb, extra) in enumerate(g):
                kt, ktb = kt_ap(kb_kind, kb, kv, base)
                out = S[:, j * NQ:(j + 1) * NQ]
                lastmm = (j == len(g) - 1)
                self.run(self.pe, lambda e, out=out, kt=kt, h=h, base=base, extra=extra: e.matmul(
                    out, lhsT=kt, rhs=QT[:, h, :], start=True, stop=(extra is None)),
                    reads=[ktb, QTb], writes=Sb, inc=(lastmm and extra is None))
                if extra is not None:
                    if extra[0] == "w":
                        ex, exb = self.wmask[:, extra[1], :], b["tbl0"]
                    else:
                        ti = h % 2
                        ex, exb = self.tbl[ti][:, extra[1] * 64:(extra[1] + 4) * 64], b["tbl%d" % ti]
                    self.run(self.pe, lambda e, out=out, ex=ex: e.matmul(out, lhsT=self.ident[:], rhs=ex, start=False, stop=True),
                             reads=[exb, b["consts"]], writes=Sb, inc=lastmm)

        def emit_exp(ui):
            h, gi, g, first, lastg = units[ui]
            n = len(g) * NQ
            S, Sb = Sap[ui % 2], psb[ui % 2]
            self.run(self.act, lambda e, S=S, n=n, ui=ui: e.activation(out=P[ui % 3][:, 0:n], in_=S[:, 0:n], func=AF.Exp),
                     reads=Sb, writes=[Pb[ui % 3]])

        def emit_pv(ui):
            h, gi, g, first, lastg = units[ui]
            kv = PERM[h] // 4
            O = self.ps_oo[h % 2][:, 0:NQ]
            Ob = ob[h % 2]
            for j, (kb_kind, kb, extra) in enumerate(g):
                va, vb = v_ap(kb_kind, kb, kv)
                rhs = P[ui % 3][:, j * NQ:(j + 1) * NQ]
                st_ = first and j == 0
                sp_ = lastg and j == len(g) - 1
                self.run(self.pe, lambda e, O=O, va=va, rhs=rhs, st_=st_, sp_=sp_: e.matmul(O[0:64, :], lhsT=va, rhs=rhs, start=st_, stop=sp_),
                         reads=[vb, Pb[ui % 3]], writes=[Ob], inc=False)
                self.run(self.pe, lambda e, O=O, rhs=rhs, st_=st_, sp_=sp_: e.matmul(O[64:128, :], lhsT=self.ones64[:, :], rhs=rhs, start=st_, stop=sp_),
                         reads=[b["consts"], Pb[ui % 3]], writes=[Ob], inc=(j == len(g) - 1))

        def emit_post(h):
            base = (h % 2) * 64
            O = self.ps_oo[h % 2][:, 0:NQ]
            Ob = ob[h % 2]
            lnd = self.A4[64:128, 0:NQ]
            rden = self.A4[64:128, NQ:2 * NQ]
            tt = self.A4[base:base + 64, (2 + h % 2) * NQ:(3 + h % 2) * NQ]
            ttb = b["A4c"] if h % 2 == 0 else b["A4d"]
            if kind == 1:
                if sink:
                    self.run(self.act, lambda e: e.activation(out=lnd, in_=O[64:128, :], func=AF.Ln, bias=self.esink[64:128, PERM[h]:PERM[h] + 1]),
                             reads=[Ob, b["esink"]], writes=[b["A4a"]])
                else:
                    self.run(self.act, lambda e: e.activation(out=lnd, in_=O[64:128, :], func=AF.Ln), reads=[Ob], writes=[b["A4a"]])
                self.run(self.act, lambda e: e.activation(out=rden, in_=lnd, func=AF.Exp, scale=-1.0), reads=[b["A4a"]], writes=[b["A4b"]])
            elif sink:
                self.run(self.dve, lambda e: e.tensor_scalar(lnd, O[64:128, :], self.esink[64:128, PERM[h]:PERM[h] + 1], None, ALU.add),
                         reads=[Ob, b["esink"]], writes=[b["A4a"]])
                self.run(self.dve, lambda e: e.reciprocal(out=rden, in_=lnd), reads=[b["A4a"]], writes=[b["A4b"]])
            else:
                self.run(self.dve, lambda e: e.reciprocal(out=rden, in_=O[64:128, :]), reads=[Ob], writes=[b["A4b"]])
            self.run(self.dve, lambda e: e.scalar_tensor_tensor(out=tt, in0=O[0:64, :], scalar=0.5, in1=rden, op0=ALU.mult, op1=ALU.mult),
                     reads=[Ob, b["A4b"]], writes=[ttb])
            self.run(self.pool, lambda e: e.tensor_tensor(out=zsT[base:base + 64, h // 2, :], in0=tt,
                                                          in1=zsT[base:base + 64, h // 2, :], op=ALU.mult),
                     reads=[ttb, zsTb], writes=[zsTb])

        nU = len(units)
        if os.environ.get('DBG_NOPIPE'):
            for ui in range(nU):
                emit_qk(ui)
                emit_exp(ui)
                emit_pv(ui)
                if units[ui][4]:
                    emit_post(units[ui][0])
            return
        emit_qk(0)
        if nU > 1:
            emit_qk(1)
        pend = None
        import math
        k_fill = max(1, math.ceil(90.0 / nU))
        for ui in range(nU):
            if filler is not None and ui >= 2:
                for _ in range(k_fill):
                    if next(filler, "done") == "done":
                        filler = None
                        break
            emit_exp(ui)
            if pend is not None:
                emit_post(pend)
                pend = None
            if ui + 2 < nU:
                emit_qk(ui + 2)
            emit_pv(ui)
            if units[ui][4]:
                pend = units[ui][0]
        emit_post(pend)


_CACHE = {}


def _constants():
    t = np.arange(TL, dtype=np.int64)
    row = (t // GRID_W).astype(np.float32)
    col = (t % GRID_W).astype(np.float32)
    inv = (np.float32(10000.0) ** (-np.arange(16, dtype=np.float32) / np.float32(16))).astype(np.float32)
    ang = np.concatenate([row[:, None] * inv, col[:, None] * inv], axis=-1).astype(np.float32)
    cos = np.cos(ang).astype(np.float32).reshape(NLT, 128, 32).transpose(1, 0, 2).copy()
    sin = np.sin(ang).astype(np.float32).reshape(NLT, 128, 32).transpose(1, 0, 2).copy()
    ident = np.eye(128, dtype=np.float32)
    wm = np.full((4, 128, NQ), NEG, dtype=np.float32)
    kk = np.arange(128)[:, None]
    qq = np.arange(128)[None, :]
    for oi, o in enumerate(range(-1, 3)):
        for j in range(2):
            d = o - j
            if d == 0:
                m = np.zeros((128, 128), np.float32)
            elif d == -1:
                m = np.where(qq <= kk, 0.0, NEG).astype(np.float32)
            elif d == 1:
                m = np.where(kk <= qq, 0.0, NEG).astype(np.float32)
            else:
                continue
            wm[oi, :, j * 128:(j + 1) * 128] = m
    cm = np.zeros((2, 128, 15, 64), dtype=np.float32)
    cc = np.arange(64)
    cs = np.clip(cc - 8, 0, 48)
    for half in range(2):
        for kc in range(64):
            p = half * 64 + kc
            colok = (kc >= cs) & (kc < cs + 16)
            for s in range(15):
                delta = (7 - s) if half == 0 else (8 - s)
                for v in range(2):
                    rowok = (-4 <= delta <= 3) if v == 0 else (-7 <= delta <= 7)
                    cm[v, p, s, :] = np.where(colok & rowok, 0.0, NEG)
    return cos, sin, ident, wm, cm.reshape(2, 128, 15 * 64)


def _tsrc(na_rel_bias):
    tab = np.asarray(na_rel_bias, dtype=np.float32)[0]
    out = np.zeros((2, NH, 15, 128), dtype=np.float32)
    rev = tab[:, :, ::-1]
    for s in range(15):
        dr0 = 14 - s
        out[0, :, s, 48:79] = rev[:, dr0, :]
        dr1 = 15 - s
        if dr1 <= 14:
            out[1, :, s, 48:79] = rev[:, dr1, :]
    return out


def _perm_w_in(w):
    idx = np.concatenate([np.arange(64) + 64 * h for h in PERM])
    cols = np.concatenate([idx, np.arange(1024, 1536), 1536 + idx])
    return np.ascontiguousarray(w[:, :, cols])


def _perm_w_out(w):
    idx = np.concatenate([np.arange(64) + 64 * h for h in PERM])
    return np.ascontiguousarray(w[:, idx, :])


def _get_prog(n_layers=NL):
    if n_layers not in _CACHE:
        p = Prog(n_layers)
        _CACHE[n_layers] = p.build()
    return _CACHE[n_layers]


def make_in_maps(inputs, n_cores=8):
    f = lambda a: np.ascontiguousarray(np.asarray(a, dtype=np.float32))
    cos, sin, ident, wm, cm = _constants()
    tsrc = _tsrc(inputs["na_rel_bias"])
    shared = {
        "w_mod": f(inputs["w_mod"]), "b_mod": f(inputs["b_mod"]), "norm_pre": f(inputs["norm_pre"]),
        "norm_post": f(inputs["norm_post"]), "w_in": _perm_w_in(f(inputs["w_in"])), "q_norm": f(inputs["q_norm"]),
        "k_norm": f(inputs["k_norm"]), "w_out": _perm_w_out(f(inputs["w_out"])), "sink": f(inputs["sink_logit"]).reshape(1, NH),
        "tsrc": tsrc, "cos_t": cos, "sin_t": sin, "ident_f": ident, "wmask_f": wm, "cmask_f": cm,
    }
    xs, xp = f(inputs["x_sample"]), f(inputs["x_prompt"])
    ck, cv = f(inputs["cache_k"]), f(inputs["cache_v"])
    c, cctx = f(inputs["c"]), f(inputs["c_ctx"])
    maps = []
    for i in range(n_cores):
        m = dict(shared)
        m["xs"] = xs[i]
        m["xp"] = xp[2 * i:2 * i + 2].reshape(TCX, D)
        m["ck"] = ck[i].reshape(NL, 256, 256)
        m["cv"] = cv[i].reshape(NL, 256, 256)
        m["cond"] = np.stack([c[i], cctx], axis=0)
        maps.append(m)
    return maps


def kernel(x_prompt, x_sample, cache_k, cache_v, c, c_ctx, w_mod, b_mod, norm_pre, norm_post,
           w_in, q_norm, k_norm, w_out, sink_logit, na_rel_bias):
    inputs = dict(x_prompt=x_prompt, x_sample=x_sample, cache_k=cache_k, cache_v=cache_v, c=c, c_ctx=c_ctx,
                  w_mod=w_mod, b_mod=b_mod, norm_pre=norm_pre, norm_post=norm_post, w_in=w_in, q_norm=q_norm,
                  k_norm=k_norm, w_out=w_out, sink_logit=sink_logit, na_rel_bias=na_rel_bias)
    nc = _get_prog(NL)
    maps = make_in_maps(inputs, 8)
    res = run_bass_kernel_spmd(nc, maps, core_ids=list(range(8)))
    r = res.results
    y_prompt = np.concatenate([r[i]["yp"].reshape(2, 256, D) for i in range(8)], axis=0).astype(np.float32)
    y_sample = np.stack([r[i]["ys"] for i in range(8)], axis=0).astype(np.float32)
    nk = np.concatenate([r[i]["nk"].reshape(2, NL, 256, NKV, HD) for i in range(8)], axis=0).astype(np.float32)
    nv = np.concatenate([r[i]["nv"].reshape(2, NL, 256, NKV, HD) for i in range(8)], axis=0).astype(np.float32)
    return (y_prompt, y_sample, nk, nv)
```

```python
import numpy as np
from contextlib import ExitStack
import concourse.bass as bass
import concourse.mybir as mybir
from concourse.bass_utils import run_bass_kernel_spmd

F32 = mybir.dt.float32
BF16 = mybir.dt.bfloat16
AF = mybir.ActivationFunctionType
ALU = mybir.AluOpType
AX = mybir.AxisListType

D = 1024
NL = 4
NH = 16
NKV = 4
HD = 64
TL = 4096
TCX = 512
NQ = 256
NLT = TL // 128
NKB = NLT + 2
EPS = 1e-6
NEG = -30000.0
GRID_W = 64
PERM = [0, 4, 1, 5, 2, 6, 3, 7, 8, 12, 9, 13, 10, 14, 11, 15]


class Sem:
    def __init__(self, h, name):
        self.h = h
        self.cnt = 0
        self.name = name


class Buf:
    def __init__(self, name):
        self.name = name
        self.w = None
        self.r = {}


class Queue:
    def __init__(self, name, sem):
        self.name = name
        self.sem = sem
        self.ops = []
        self.seen = {}
        self.pending = False

    def wait(self, toks):
        for t in toks:
            if t is None:
                continue
            sem, val = t
            if sem is self.sem and (val > sem.cnt or self.name == "pe"):
                continue
            if self.seen.get(sem, 0) >= val:
                continue
            self.seen[sem] = val
            self.ops.append(("wait", sem, val))


def _round_robin(gens):
    gens = list(gens)
    while gens:
        for g in list(gens):
            try:
                yield next(g)
            except StopIteration:
                gens.remove(g)


def _mark(reads, writes, tok):
    for b in reads:
        if b.r.get(tok[0], 0) < tok[1]:
            b.r[tok[0]] = tok[1]
    for b in writes:
        b.w = tok
        b.r = {}


def _deps(reads, writes):
    deps = []
    for b in reads:
        deps.append(b.w)
    for b in writes:
        deps.append(b.w)
        deps.extend(b.r.items())
    return deps


class Prog:
    def __init__(self, n_layers=NL):
        self.n_layers = n_layers
        self.nc = bass.Bass("TRN2", target_bir_lowering=False)
        self.es = ExitStack()
        self.sems = []
        self.dma_sems = []

    def sem(self, name):
        s = Sem(self.es.enter_context(self.nc.semaphore(name)), name)
        self.sems.append(s)
        return s

    def dsem(self, name):
        s = self.sem(name)
        self.dma_sems.append(s)
        return s

    def sb(self, name, shape, dt):
        return self.es.enter_context(self.nc.sbuf_tensor(name, shape, dt))

    def psum(self, name, shape, dt):
        return self.es.enter_context(self.nc.psum_tensor(name, shape, dt))

    def run(self, q, fn, reads=(), writes=(), inc=True):
        q.wait(_deps(reads, writes))
        if inc:
            q.sem.cnt += 1
            tok = (q.sem, q.sem.cnt)
            q.ops.append(("op", fn, True))
            q.pending = False
        else:
            tok = (q.sem, q.sem.cnt + 1)
            q.ops.append(("op", fn, False))
            q.pending = True
        _mark(reads, writes, tok)
        return tok

    def dma(self, q, out, in_, sem, reads=(), writes=(), **kw):
        if isinstance(sem, str):
            if sem not in self.ds:
                self.ds[sem] = self.dsem("d_" + sem)
            sem = self.ds[sem]
        q.wait(_deps(reads, writes))
        sem.cnt += 16
        tok = (sem, sem.cnt)
        q.ops.append(("dma", out, in_, sem, kw))
        _mark(reads, writes, tok)
        return tok

    def barrier(self):
        toks = []
        for q in self.cq:
            assert not q.pending, q.name
            toks.append((q.sem, q.sem.cnt))
        for s in self.dma_sems:
            toks.append((s, s.cnt))
        for q in self.allq:
            q.wait(toks)

    def emit(self, q, e):
        for op in q.ops:
            if op[0] == "wait":
                e.wait_ge(op[1].h, op[2])
            elif op[0] == "op":
                ins = op[1](e)
                if op[2]:
                    ins.then_inc(q.sem.h, 1)
            else:
                e.dma_start(out=op[1], in_=op[2], **op[4]).then_inc(op[3].h, 16)

    def build(self):
        nc = self.nc
        dram = lambda n, s, k="ExternalInput", dt=F32: nc.dram_tensor(n, list(s), dt, kind=k)
        self.xs = dram("xs", (TL, D))
        self.xp = dram("xp", (TCX, D))
        self.ck = dram("ck", (NL, 256, 256))
        self.cv = dram("cv", (NL, 256, 256))
        self.cond = dram("cond", (2, D))
        self.w_mod = dram("w_mod", (NL, D, 3 * D))
        self.b_mod = dram("b_mod", (NL, 3 * D))
        self.norm_pre = dram("norm_pre", (NL, D))
        self.norm_post = dram("norm_post", (NL, D))
        self.w_in = dram("w_in", (NL, D, 2560))
        self.q_norm = dram("q_norm", (NL, HD))
        self.k_norm = dram("k_norm", (NL, HD))
        self.w_out = dram("w_out", (NL, D, D))
        self.sink = dram("sink", (1, NH))
        self.tsrc = dram("tsrc", (2, NH, 15, 128))
        self.cos_d = dram("cos_t", (128, NLT, 32))
        self.sin_d = dram("sin_t", (128, NLT, 32))
        self.ident_d = dram("ident_f", (128, 128))
        self.wmask_d = dram("wmask_f", (4, 128, NQ))
        self.cmask_d = dram("cmask_f", (2, 128, 15 * 64))
        self.ys = dram("ys", (TL, D), "ExternalOutput")
        self.yp = dram("yp", (TCX, D), "ExternalOutput")
        self.nk = dram("nk", (2, NL, 256, 256), "ExternalOutput")
        self.nv = dram("nv", (2, NL, 256, 256), "ExternalOutput")
        self.hs_d = dram("hs_scr", (TL, D), "Internal")
        self.hp_d = dram("hp_scr", (TCX, D), "Internal")
        self.modrow = dram("modrow", (NL, 2, 3 * D), "Internal")
        self.tzD = dram("tz_scr", (128, NH, 15 * 64), "Internal")
        self.nab = dram("nab_scr", (2, NH, 128, 15 * 64), "Internal", BF16)

        self.pe = Queue("pe", self.sem("s_pe"))
        self.act = Queue("act", self.sem("s_act"))
        self.dve = Queue("dve", self.sem("s_dve"))
        self.pool = Queue("pool", self.sem("s_pool"))
        self.sp = Queue("sp", self.sem("s_sp"))
        self.cq = [self.pe, self.act, self.dve, self.pool]
        self.allq = [self.pe, self.act, self.dve, self.pool, self.sp]

        sb, ps = self.sb, self.psum
        self.win = sb("win", [128, 8, 2560], BF16)
        self.wout = sb("wout", [128, 8, 1024], BF16)
        self.KT = sb("KT", [128, 2, NKB * 128], BF16)
        self.VV = sb("VV", [128, NKB, NKV, HD], BF16)
        self.KTc = self.KT
        self.VVc = self.VV
        self.mods = sb("mods", [128, 3 * D], F32)
        self.cosT = sb("cosT", [128, NLT, 32], F32)
        self.sinT = sb("sinT", [128, NLT, 32], F32)
        self.ident = sb("ident", [128, 128], BF16)
        self.ones64 = sb("ones64", [128, 64], BF16)
        self.A1 = sb("A1", [128, 1024], F32)
        self.A2 = sb("A2", [128, 1024], F32)
        self.A3 = sb("A3", [128, 1024], F32)
        self.A4 = sb("A4", [128, 1024], F32)
        self.B1 = sb("B1", [128, 1024], F32)
        self.B2 = sb("B2", [128, 1024], F32)
        self.hB = [sb("hB0", [128, 1024], F32), sb("hB1", [128, 1024], F32)]
        self.u = sb("u", [128, 1024], BF16)
        self.u1 = sb("u1", [128, 1024], BF16)
        self.P3 = sb("P3", [128, 1024], BF16)
        self.uTg = sb("uTg", [128, 8, NQ], BF16)
        self.QTs = [sb("QT0", [128, NH, NQ], BF16), sb("QT1", [128, NH, NQ], BF16)]
        self.zsTs = [sb("zsT0", [128, 8, NQ], BF16), sb("zsT1", [128, 8, NQ], BF16)]
        self.hR = [sb("hR0", [128, 1024], F32), sb("hR1", [128, 1024], F32)]
        self.R3 = sb("R3", [128, 1024], F32)
        self.R4 = sb("R4", [128, 1024], F32)
        self.S3 = sb("S3", [128, 1024], F32)
        self.S4 = sb("S4", [128, 1024], F32)
        self.kd = sb("kd", [128, NKV, HD], BF16)
        self.kd1 = sb("kd1", [128, NKV, HD], BF16)
        self.kv32 = self.R3[:, :].rearrange("p (i c) -> p i c", i=2)
        self.tbl = [sb("tbl0", [128, 1024], BF16), sb("tbl1", [128, 1024], BF16)]
        self.wmask = self.tbl[0][:, :].rearrange("p (o q) -> p o q", q=NQ)
        self.stat = sb("stat", [128, 256], F32)
        self.gq = sb("gq", [128, HD], F32)
        self.gk = sb("gk", [128, HD], F32)
        self.esink = sb("esink", [128, NH], F32)
        self.mhalf = sb("mhalf", [128, 16], F32)
        self.condT = sb("condT", [128, 2, 8], F32)
        self.scT = sb("scT", [128, 8, 2], F32)
        self.ps_s = [ps("ps_s0", [128, 1024], F32), ps("ps_s1", [128, 1024], F32)]
        self.ps_oo = [ps("ps_o0", [128, 512], F32), ps("ps_o1", [128, 512], F32)]
        self.ps_p = ps("ps_p", [128, 1024], F32)
        self.ps_t = self.ps_p[:, 512:1024].bitcast(BF16)

        B = Buf
        self.b = {n: B(n) for n in [
            "win", "wout", "mods", "consts", "A1", "A2", "A3a", "A3b", "A4a", "A4b", "A4c", "A4d",
            "hB0", "hB1", "hR0", "hR1", "R3a", "R3b", "R4a", "R4b", "S3a", "S3b", "S4a", "S4b", "B1", "B2", "u", "u1", "kd1", "P3", "uTg0", "uTg1", "pb0", "pb1", "pb2", "pb3", "st_ln1", "st_q1", "st_y1", "QT0", "QT1", "zsT0", "zsT1", "kd", "tbl0", "tbl1", "gq", "gk",
            "esink", "condT", "scT", "ps_s0", "ps_s1", "ps_o0", "ps_o1", "ps_t", "ps_p0", "ps_p1",
            "st_ln", "st_q", "st_k", "st_y", "modrow", "tzD", "nab"]}
        self.b["ps_t"] = self.b["ps_p1"]
        self.mk_scratch_sets()
        self.b["kv32a"] = self.b["R3a"]
        self.b["kv32b"] = self.b["R3b"]
        self.bKT = [B("KT%d" % i) for i in range(NKB)]
        self.bVV = [B("VV%d" % i) for i in range(NKB)]
        self.bKTc = self.bKT
        self.bVVc = self.bVV
        self.bhs = [B("hs%d" % i) for i in range(NLT)]
        self.bhp = [B("hp%d" % i) for i in range(4)]
        self.bout = B("outkv")
        self.ds = {}

        self.prologue()
        for l in range(self.n_layers):
            self.layer(l)
        self.barrier()

        blk = self.es.enter_context(nc.Block())

        @blk.tensor
        def _(e):
            self.emit(self.pe, e)

        @blk.scalar
        def _(e):
            self.emit(self.act, e)

        @blk.vector
        def _(e):
            self.emit(self.dve, e)

        @blk.gpsimd
        def _(e):
            self.emit(self.pool, e)

        @blk.sync
        def _(e):
            self.emit(self.sp, e)

        self.es.close()
        return nc

    def mk_scratch_sets(self):
        b = self.b

        class SC:
            pass
        self.sc = []
        for k in range(2):
            sc = SC()
            sc.k = k
            sc.A1, sc.bA1 = (self.A1, b["A1"]) if k == 0 else (self.B1, b["B1"])
            sc.A2, sc.bA2 = (self.A2, b["A2"]) if k == 0 else (self.B2, b["B2"])
            sc.kA2 = "A2" if k == 0 else "B2"
            sc.R3, sc.bR3a, sc.bR3b = (self.R3, b["R3a"], b["R3b"]) if k == 0 else (self.S3, b["S3a"], b["S3b"])
            sc.kR3 = "R3a" if k == 0 else "S3a"
            sc.R4, sc.bR4a, sc.bR4b = (self.R4, b["R4a"], b["R4b"]) if k == 0 else (self.S4, b["S4a"], b["S4b"])
            sc.u, sc.bu = (self.u, b["u"]) if k == 0 else (self.u1, b["u1"])
            sc.kd, sc.bkd = (self.kd, b["kd"]) if k == 0 else (self.kd1, b["kd1"])
            sc.hB, sc.bhB, sc.khB = self.hB[k], b["hB%d" % k], "hB%d" % k
            sc.hR, sc.bhR, sc.khR = self.hR[k], b["hR%d" % k], "hR%d" % k
            sc.uT, sc.buT = self.uTg[:, :, k * 128:(k + 1) * 128], b["uTg%d" % k]
            sc.so = 128 * k
            sc.bst_ln = b["st_ln"] if k == 0 else b["st_ln1"]
            sc.bst_q = b["st_q"] if k == 0 else b["st_q1"]
            sc.bst_y = b["st_y"] if k == 0 else b["st_y1"]
            self.sc.append(sc)
        self.use_psum(0, own=True)
        self.use_psum(1, own=True)

    def use_psum(self, k, own):
        b = self.b
        sc = self.sc[k]
        if k == 0 or not own:
            sc.ps_p, sc.bps = self.ps_p, [b["ps_p0"], b["ps_p1"]]
        else:
            sc.ps_p, sc.bps = self.ps_s[0], [b["pb0"], b["pb1"]]
        sc.ps_t, sc.bpt = sc.ps_p[:, 512:1024].bitcast(BF16), sc.bps[1]

    def prologue(self):
        b, ds = self.b, self.ds
        c = [b["consts"]]
        self.dma(self.sp, self.cosT[:], self.cos_d.ap()[:, :, :], "consts", writes=c)
        self.dma(self.sp, self.sinT[:], self.sin_d.ap()[:, :, :], "consts", writes=c)
        self.dma(self.pool, self.ident[:], self.ident_d.ap()[:, :], "sw_consts", writes=c)
        self.run(self.pool, lambda e: e.memset(self.ones64[:], 1.0), writes=c)
        self.run(self.pool, lambda e: e.memset(self.mhalf[:], -0.5), writes=c)
        for i_ in range(2):
            self.run(self.pool, lambda e, i_=i_: e.memset(self.QTs[i_][:], 0.0), writes=[b["QT%d" % i_]])
        self.dma(self.sp, self.esink[:], self.sink.ap()[0:1, :].partition_broadcast(128), "esink", writes=[b["esink"]])
        self.run(self.act, lambda e: e.activation(out=self.esink[:], in_=self.esink[:], func=AF.Exp),
                 reads=[b["esink"]], writes=[b["esink"]])
        for j in range(2):
            self.dma(self.sp, self.condT[:, j, :], bass.AP(self.cond, j * D, [[1, 128], [128, 8]]), "condT",
                     writes=[b["condT"]], allow_slow_non_contiguous=True)
        self.run(self.act, lambda e: e.activation(out=self.scT[:, :, :].rearrange("p k j -> p j k"), in_=self.condT[:, :, :], func=AF.Silu),
                 reads=[b["condT"]], writes=[b["scT"]])
        self.load_weights(0)
        if self.n_layers > 2:
            self.na_tables()
        self.barrier()

    def na_tables(self):
        b, ds = self.b, self.ds
        for half in range(2):
            for kc in range(64):
                p = half * 64 + kc
                self.dma(self.sp, self.tzD.ap()[p:p + 1, :, :].rearrange("p h (s c) -> p (h s) c", c=64),
                         bass.AP(self.tsrc, half * NH * 15 * 128 + 63 - kc, [[0, 1], [128, NH * 15], [1, 64]]),
                         "tzD")
        b["tzD"].w = (self.ds["tzD"], self.ds["tzD"].cnt)
        b["tzD"].r = {}
        cm = [self.A2, self.A3]
        self.dma(self.sp, self.A2[:, 0:960], self.cmask_d.ap()[0], "A2", writes=[b["A2"]])
        self.dma(self.sp, self.A3[:, 0:960], self.cmask_d.ap()[1], "A3a", writes=[b["A3a"], b["A3b"]])
        cmb = [[b["A2"]], [b["A3a"], b["A3b"]]]
        ob = self.A4[:, 0:480].bitcast(BF16)
        for h in range(NH):
            self.dma(self.sp, self.A1[:, 0:960], self.tzD.ap()[:, h, :], "A1", reads=[b["tzD"]], writes=[b["A1"]])
            for v in range(2):
                self.run(self.dve, lambda e, v=v: e.tensor_tensor(out=ob, in0=self.A1[:, 0:960], in1=cm[v][:, 0:960], op=ALU.add),
                         reads=[b["A1"]] + cmb[v], writes=[b["A4a"], b["A4b"]])
                self.dma(self.sp, self.nab.ap()[v, h], ob, "A4a", reads=[b["A4a"], b["A4b"]], writes=[b["nab"]])

    def load_weights(self, l):
        b = self.b
        for kc in range(8):
            self.dma(self.pool, self.win[:, kc, :], self.w_in.ap()[l, kc * 128:(kc + 1) * 128, :], "win", writes=[b["win"]])
        for kc in range(8):
            self.dma(self.pool, self.wout[:, kc, :], self.w_out.ap()[l, kc * 128:(kc + 1) * 128, :], "wout", writes=[b["wout"]])

    def layer(self, l):
        b, ds = self.b, self.ds
        kind = l % 3
        if l > 0:
            self.load_weights(l)
        self.dma(self.sp, self.gq[:], self.q_norm.ap()[l:l + 1, :].partition_broadcast(128), "gq", writes=[b["gq"]])
        self.dma(self.sp, self.gk[:], self.k_norm.ap()[l:l + 1, :].partition_broadcast(128), "gk", writes=[b["gk"]])
        self.run(self.dve, lambda e: e.tensor_scalar(self.gq[:], self.gq[:], HD ** -0.5, None, ALU.mult),
                 reads=[b["gq"]], writes=[b["gq"]])
        import os
        stg = int(os.environ.get("DBG_STAGE", "99"))
        nch = int(os.environ.get("DBG_NCH", str(TL // NQ)))
        if stg < 2:
            return
        if kind == 1:
            self.dma(self.pool, self.wmask, self.wmask_d.ap().rearrange("o p q -> p o q"), "sw_tbl0", writes=[b["tbl0"]])
        self.modulation_rows(l)
        if stg < 3:
            return
        self.load_mods(l, 1)
        self.phase_a(l, 4, ctx=True)
        if stg < 4:
            return
        self.phase_b(l, 2, ctx=True)
        if stg < 5:
            return
        self.load_mods(l, 0)
        self.cached_kv(l)
        self.phase_a(l, NLT, ctx=False)
        if stg < 6:
            return
        self.phase_b(l, nch, ctx=False)

    def modulation_rows(self, l):
        b, ds = self.b, self.ds
        stg = [self.A1, self.A2, self.hB[0], self.hB[1]]
        wm = [t_[:, :].rearrange("p (k n) -> p k n", n=128) for t_ in stg]
        wmb = [b["A1"], b["A2"], b["hB0"], b["hB1"]]
        wmk = ["A1", "A2", "hB0", "hB1"]
        NCH = 3 * D // 128
        for ch in range(NCH):
            i = ch % 4
            self.dma(self.sp, wm[i], self.w_mod.ap()[l, :, ch * 128:(ch + 1) * 128].rearrange("(k p) n -> p k n", p=128),
                     wmk[i], writes=[wmb[i]])
            pv = self.ps_p[0:2, (ch % 4) * 128:(ch % 4) * 128 + 128]
            pb = b["ps_p0"]
            for kc in range(8):
                self.run(self.pe, lambda e, kc=kc, i=i, pv=pv: e.matmul(pv, lhsT=self.scT[:, kc, :], rhs=wm[i][:, kc, :],
                                                                   start=(kc == 0), stop=(kc == 7)),
                         reads=[wmb[i], b["scT"]], writes=[pb], inc=(kc == 7))
            if ch % 4 == 3:
                c0 = (ch - 3) * 128
                bm = self.A3[0:2, 0:512]
                mo = self.A3[0:2, 512:1024]
                self.dma(self.sp, bm, self.b_mod.ap()[l:l + 1, c0:c0 + 512].partition_broadcast(2), "A3a", writes=[b["A3a"]])
                self.run(self.dve, lambda e, bm=bm, mo=mo: e.tensor_tensor(out=mo, in0=self.ps_p[0:2, 0:512], in1=bm, op=ALU.add),
                         reads=[pb, b["A3a"]], writes=[b["A3b"]])
                self.dma(self.sp, self.modrow.ap()[l, :, c0:c0 + 512], mo, "A3b", reads=[b["A3b"]], writes=[b["modrow"]])

    def load_mods(self, l, j):
        b, ds = self.b, self.ds
        self.dma(self.sp, self.mods[:], self.modrow.ap()[l, j:j + 1, :].partition_broadcast(128), "mods",
                 reads=[b["modrow"]], writes=[b["mods"]])
        self.dma(self.sp, self.A1[:], self.norm_pre.ap()[l:l + 1, :].partition_broadcast(128), "A1", writes=[b["A1"]])
        self.dma(self.sp, self.A2[:], self.norm_post.ap()[l:l + 1, :].partition_broadcast(128), "A2", writes=[b["A2"]])
        self.run(self.dve, lambda e: e.scalar_tensor_tensor(out=self.mods[:, D:2 * D], in0=self.mods[:, D:2 * D], scalar=1.0,
                                                            in1=self.A1[:], op0=ALU.add, op1=ALU.mult),
                 reads=[b["mods"], b["A1"]], writes=[b["mods"]])
        self.run(self.dve, lambda e: e.tensor_tensor(out=self.mods[:, 2 * D:3 * D], in0=self.mods[:, 2 * D:3 * D], in1=self.A2[:], op=ALU.mult),
                 reads=[b["mods"], b["A2"]], writes=[b["mods"]])

    def rstd_small(self, ss, tmp, out, n, inv_n, bufs):
        self.run(self.pool, lambda e: e.tensor_scalar(tmp, ss, inv_n, EPS, ALU.mult, ALU.add), reads=bufs, writes=bufs)
        self.run(self.pool, lambda e: e.tensor_tensor(out=out, in0=tmp, in1=self.mhalf[:, 0:n], op=ALU.pow),
                 reads=bufs + [self.b["consts"]], writes=bufs)

    def ln_u_g(self, sc, hbuf, hB):
        ps_p, ps_t, bps, bpt = sc.ps_p, sc.ps_t, sc.bps, sc.bpt
        b = self.b
        st = self.stat
        o = sc.so
        sb_ = [sc.bst_ln]
        self.run(self.act, lambda e: e.activation(out=sc.A1[:], in_=hbuf[:], func=AF.Square, accum_out=st[:, o:o + 1]),
                 reads=[hB], writes=[sc.bA1, sc.bst_ln])
        yield
        self.rstd_small(st[:, o:o + 1], st[:, o + 1:o + 2], st[:, o + 2:o + 3], 1, 1.0 / D, sb_)
        yield
        self.run(self.dve, lambda e: e.scalar_tensor_tensor(out=sc.A2[:], in0=hbuf[:], scalar=st[:, o + 2:o + 3], in1=self.mods[:, D:2 * D],
                                                            op0=ALU.mult, op1=ALU.mult),
                 reads=[hB, sc.bst_ln, b["mods"]], writes=[sc.bA2])
        yield
        self.run(self.dve, lambda e: e.tensor_tensor(out=sc.u[:], in0=sc.A2[:], in1=self.mods[:, 0:D], op=ALU.add),
                 reads=[sc.bA2, b["mods"]], writes=[sc.bu])
        yield
        for kc in range(8):
            self.run(self.pe, lambda e, kc=kc: e.transpose(ps_t[:, kc * 128:(kc + 1) * 128], sc.u[:, kc * 128:(kc + 1) * 128], self.ident[:]),
                     reads=[sc.bu, b["consts"]], writes=[bpt], inc=(kc == 7))
        yield
        self.run(self.dve, lambda e: e.tensor_copy(out=sc.uT, in_=ps_t[:, :].rearrange("p (k t) -> p k t", t=128)),
                 reads=[bpt], writes=[sc.buT])
        yield

    def prep_heads_g(self, sc, src, nh, g, rope_tile, sbuf, outs):
        b = self.b
        RT, rba, rbb = sc.R3, sc.bR3a, sc.bR3b
        W = nh * HD
        st = self.stat
        o = sc.so
        if nh == NH:
            ss, ln, rs = st[:, o + 8:o + 24], st[:, o + 24:o + 40], st[:, o + 40:o + 56]
        else:
            ss, ln, rs = st[:, o + 60:o + 64], st[:, o + 64:o + 68], st[:, o + 68:o + 72]
        sq = sc.A1[:, 0:W]
        xn = sc.A2[:, 0:W]
        v3 = lambda ap: ap.rearrange("p (h d) -> p h d", d=HD)
        self.run(self.act, lambda e: e.activation(out=sq, in_=src, func=AF.Square), reads=sbuf, writes=[sc.bA1])
        yield
        self.run(self.dve, lambda e: e.tensor_reduce(out=ss, in_=v3(sq), axis=AX.X, op=ALU.add), reads=[sc.bA1], writes=[sc.bst_q])
        yield
        self.rstd_small(ss, ln, rs, nh, 1.0 / HD, [sc.bst_q])
        yield
        self.run(self.dve, lambda e: e.tensor_tensor(out=v3(xn), in0=v3(src), in1=g[:, :].unsqueeze(1).to_broadcast([128, nh, HD]), op=ALU.mult),
                 reads=sbuf + [b["gq"], b["gk"]], writes=[sc.bA2])
        yield
        fin, finb = xn, sc.bA2
        if rope_tile is not None:
            cosb = self.cosT[:, rope_tile, :].unsqueeze(1).to_broadcast([128, nh, 32])
            sinb = self.sinT[:, rope_tile, :].unsqueeze(1).to_broadcast([128, nh, 32])
            H = nh * 32
            ra = RT[:, 0:H].rearrange("p (h d) -> p h d", d=32)
            rb = RT[:, 512:512 + H].rearrange("p (h d) -> p h d", d=32)
            x1, x2 = v3(xn)[:, :, 0:32], v3(xn)[:, :, 32:64]
            xr = sc.A1[:, 0:W]
            o1, o2 = v3(xr)[:, :, 0:32], v3(xr)[:, :, 32:64]
            tt = lambda out, i0, i1, op, rd, wr: self.run(
                self.dve, lambda e: e.tensor_tensor(out=out, in0=i0, in1=i1, op=op), reads=rd, writes=wr)
            cb = [b["consts"]]
            tt(ra, x1, cosb, ALU.mult, [sc.bA2] + cb, [rba])
            tt(rb, x2, sinb, ALU.mult, [sc.bA2] + cb, [rbb])
            yield
            tt(o1, ra, rb, ALU.subtract, [rba, rbb], [sc.bA1])
            yield
            tt(ra, x2, cosb, ALU.mult, [sc.bA2] + cb, [rba])
            tt(rb, x1, sinb, ALU.mult, [sc.bA2] + cb, [rbb])
            yield
            tt(o2, ra, rb, ALU.add, [rba, rbb], [sc.bA1])
            yield
            fin, finb = xr, sc.bA1
        for (oap, wb) in outs:
            i0 = v3(fin)
            i1 = rs.unsqueeze(2).to_broadcast([128, nh, HD])
            self.run(self.dve, lambda e, oap=oap, i0=i0, i1=i1: e.tensor_tensor(out=oap, in0=i0, in1=i1, op=ALU.mult),
                     reads=[finb, sc.bst_q], writes=wb)
            yield

    def kt_store(self, sc, kdst, kbuf):
        ps_p, ps_t, bps, bpt = sc.ps_p, sc.ps_t, sc.bps, sc.bpt
        b = self.b
        for kp in range(2):
            self.run(self.pe, lambda e, kp=kp: e.transpose(ps_t[:, kp * 128:(kp + 1) * 128],
                                                      sc.kd[:, 2 * kp:2 * kp + 2, :].rearrange("p a d -> p (a d)"), self.ident[:]),
                     reads=[sc.bkd, b["consts"]], writes=[bpt], inc=(kp == 1))
        self.run(self.dve, lambda e: e.tensor_copy(out=kdst, in_=ps_t[:, 0:256].rearrange("p (k t) -> p k t", t=128)),
                 reads=[bpt], writes=[kbuf])

    def cached_kv(self, l):
        b = self.b
        for j in range(2):
            sc = self.sc[j]
            ps_p, ps_t, bps, bpt = sc.ps_p, sc.ps_t, sc.bps, sc.bpt
            kb = NLT + j
            hb_, hbuf = sc.bhB, sc.hB
            self.dma(self.sp, hbuf[:, 0:256], self.ck.ap()[l, j * 128:(j + 1) * 128, :], sc.khB, writes=[hb_])
            self.dma(self.sp, hbuf[:, 256:512], self.cv.ap()[l, j * 128:(j + 1) * 128, :], sc.khB, writes=[hb_])
            self.run(self.dve, lambda e, hbuf=hbuf, sc=sc: e.tensor_copy(out=sc.kd[:, :, :].rearrange("p h d -> p (h d)"), in_=hbuf[:, 0:256]),
                     reads=[hb_], writes=[sc.bkd])
            self.kt_store(sc, self.KT[:, :, kb * 128:(kb + 1) * 128], self.bKT[kb])
            self.run(self.act, lambda e, kb=kb, hbuf=hbuf: e.activation(out=self.VV[:, kb, :, :].rearrange("p h d -> p (h d)"),
                                                                       in_=hbuf[:, 256:512], func=AF.Copy),
                     reads=[hb_], writes=[self.bVV[kb]])

    def phase_a_tile_g(self, l, t, ctx, sc):
        ps_p, ps_t, bps, bpt = sc.ps_p, sc.ps_t, sc.bps, sc.bpt
        b = self.b
        kind = l % 3
        hbuf, hB = sc.hB, sc.bhB
        src, hd = self._hsrc(l, t, ctx)
        self.dma(self.sp, hbuf[:], src, sc.khB, reads=[hd], writes=[hB])
        yield
        yield from self.ln_u_g(sc, hbuf, hB)
        pk = ps_p[:, 0:512]
        for kc in range(8):
            self.run(self.pe, lambda e, kc=kc: e.matmul(pk, lhsT=sc.uT[:, kc, :], rhs=self.win[:, kc, 1024:1536],
                                                   start=(kc == 0), stop=(kc == 7)),
                     reads=[sc.buT, b["win"]], writes=[bps[0]], inc=(kc == 7))
        yield
        rope = (not ctx) and kind != 2
        outs = [(sc.kd[:], [sc.bkd])]
        kv32 = sc.R3[:, 0:512]
        if ctx:
            outs.append((kv32[:, 0:256].rearrange("p (h d) -> p h d", d=HD), [sc.bR3a]))
        yield from self.prep_heads_g(sc, ps_p[:, 0:256], NKV, self.gk, t if rope else None, [bps[0]], outs)
        self.kt_store(sc, self.KT[:, :, t * 128:(t + 1) * 128], self.bKT[t])
        yield
        self.run(self.act, lambda e: e.activation(out=self.VV[:, t, :, :].rearrange("p h d -> p (h d)"), in_=ps_p[:, 256:512], func=AF.Copy),
                 reads=[bps[0]], writes=[self.bVV[t]])
        yield
        if ctx:
            self.run(self.act, lambda e: e.activation(out=kv32[:, 256:512], in_=ps_p[:, 256:512], func=AF.Copy),
                     reads=[bps[0]], writes=[sc.bR3a])
            s_, r0 = t // 2, (t % 2) * 128
            self.dma(self.sp, self.nk.ap()[s_, l, r0:r0 + 128, :], kv32[:, 0:256], sc.kR3, reads=[sc.bR3a], writes=[self.bout])
            self.dma(self.sp, self.nv.ap()[s_, l, r0:r0 + 128, :], kv32[:, 256:512], sc.kR3, reads=[sc.bR3a], writes=[self.bout])
            yield

    def phase_a(self, l, ntiles, ctx):
        self.use_psum(1, own=True)
        for t in range(0, ntiles, 2):
            gens = [self.phase_a_tile_g(l, t, ctx, self.sc[0]), self.phase_a_tile_g(l, t + 1, ctx, self.sc[1])]
            for _ in _round_robin(gens):
                pass

    def _hsrc(self, l, t, ctx):
        if ctx:
            return (self.xp if l == 0 else self.hp_d).ap()[t * 128:(t + 1) * 128, :], self.bhp[t]
        return (self.xs if l == 0 else self.hs_d).ap()[t * 128:(t + 1) * 128, :], self.bhs[t]

    def stage1_tile_g(self, l, c, j, ctx, sc):
        ps_p, ps_t, bps, bpt = sc.ps_p, sc.ps_t, sc.bps, sc.bpt
        b = self.b
        kind = l % 3
        si = c % 2
        QT, QTb = self.QTs[si], b["QT%d" % si]
        zsT, zsTb = self.zsTs[si], b["zsT%d" % si]
        ppb = bps
        t = 2 * c + j
        hbuf, hB = sc.hB, sc.bhB
        src, hd = self._hsrc(l, t, ctx)
        self.dma(self.sp, hbuf[:], src, sc.khB, reads=[hd], writes=[hB])
        yield
        yield from self.ln_u_g(sc, hbuf, hB)
        for (c0, isq) in ((0, True), (1536, False)):
            for n in range(2):
                for kc in range(8):
                    self.run(self.pe, lambda e, kc=kc, n=n, c0=c0: e.matmul(
                        ps_p[:, n * 512:(n + 1) * 512], lhsT=sc.uT[:, kc, :],
                        rhs=self.win[:, kc, c0 + n * 512:c0 + (n + 1) * 512], start=(kc == 0), stop=(kc == 7)),
                        reads=[sc.buT, b["win"]], writes=[ppb[n]], inc=(kc == 7))
                yield
            if isq:
                rope = (not ctx) and kind != 2
                qr = sc.R4[:, 0:512].bitcast(BF16)
                srcb = [sc.bR4a]
                yield from self.prep_heads_g(sc, ps_p[:, :], NH, self.gq, t if rope else None, ppb,
                                             [(qr.rearrange("p (h d) -> p h d", d=HD), srcb)])
                srcT = qr
            else:
                zs = sc.R4[:, 512:1024].bitcast(BF16)
                srcb = [sc.bR4b]
                self.run(self.act, lambda e: e.activation(out=sc.A1[:], in_=ps_p[:, :], func=AF.Tanh, scale=0.5),
                         reads=ppb, writes=[sc.bA1])
                yield
                self.run(self.dve, lambda e, zs=zs: e.scalar_tensor_tensor(out=zs, in0=sc.A1[:], scalar=1.0, in1=ps_p[:, :],
                                                                      op0=ALU.add, op1=ALU.mult),
                         reads=ppb + [sc.bA1], writes=srcb)
                yield
                srcT = zs
            for pr in range(8):
                self.run(self.pe, lambda e, pr=pr, srcT=srcT: e.transpose(ps_t[:, pr * 128:(pr + 1) * 128],
                                                                     srcT[:, pr * 128:(pr + 1) * 128], self.ident[:]),
                         reads=srcb + [b["consts"]], writes=[bpt], inc=(pr == 7))
            yield
            if isq:
                for par in range(2):
                    qo = QT[par * 64:(par + 1) * 64, :, :].rearrange("p (pr two) q -> p pr two q", two=2)[:, :, par, j * 128:(j + 1) * 128]
                    self.run(self.dve, lambda e, qo=qo, par=par: e.tensor_copy(
                        out=qo, in_=ps_t[par * 64:(par + 1) * 64, :].rearrange("p (k t) -> p k t", t=128)),
                        reads=[bpt], writes=[QTb])
            else:
                self.run(self.dve, lambda e: e.tensor_copy(out=zsT[:, :, j * 128:(j + 1) * 128],
                                                          in_=ps_t[:, :].rearrange("p (k t) -> p k t", t=128)),
                         reads=[bpt], writes=[zsTb])
            yield

    def stage3_tile_g(self, l, c, j, ctx, sc):
        ps_p, ps_t, bps, bpt = sc.ps_p, sc.ps_t, sc.bps, sc.bpt
        b = self.b
        si = c % 2
        gT, gTb = self.zsTs[si], b["zsT%d" % si]
        last = (l == self.n_layers - 1)
        ppb = bps
        st = self.stat
        o = sc.so
        t = 2 * c + j
        hbuf, hB = sc.hR, sc.bhR
        src, hd = self._hsrc(l, t, ctx)
        self.dma(self.sp, hbuf[:], src, sc.khR, reads=[hd], writes=[hB])
        yield
        for n in range(2):
            for pr in range(8):
                self.run(self.pe, lambda e, pr=pr, n=n: e.matmul(
                    ps_p[:, n * 512:(n + 1) * 512], lhsT=gT[:, pr, j * 128:(j + 1) * 128],
                    rhs=self.wout[:, pr, n * 512:(n + 1) * 512], start=(pr == 0), stop=(pr == 7)),
                    reads=[gTb, b["wout"]], writes=[ppb[n]], inc=(pr == 7))
            yield
        self.run(self.act, lambda e: e.activation(out=sc.A1[:], in_=ps_p[:, :], func=AF.Square, accum_out=st[:, o + 80:o + 81]),
                 reads=ppb, writes=[sc.bA1, sc.bst_y])
        yield
        self.rstd_small(st[:, o + 80:o + 81], st[:, o + 81:o + 82], st[:, o + 82:o + 83], 1, 1.0 / D, [sc.bst_y])
        yield
        self.run(self.dve, lambda e: e.scalar_tensor_tensor(out=sc.A1[:], in0=ps_p[:, :], scalar=st[:, o + 82:o + 83],
                                                            in1=self.mods[:, 2 * D:3 * D], op0=ALU.mult, op1=ALU.mult),
                 reads=ppb + [sc.bst_y, b["mods"]], writes=[sc.bA1])
        yield
        self.run(self.dve, lambda e: e.tensor_tensor(out=sc.A2[:], in0=sc.A1[:], in1=hbuf[:], op=ALU.add),
                 reads=[sc.bA1, hB], writes=[sc.bA2])
        yield
        if ctx:
            dst = (self.yp if last else self.hp_d).ap()[t * 128:(t + 1) * 128, :]
            hd = self.bhp[t]
        else:
            dst = (self.ys if last else self.hs_d).ap()[t * 128:(t + 1) * 128, :]
            hd = self.bhs[t]
        self.dma(self.sp, dst, sc.A2[:], sc.kA2, reads=[sc.bA2], writes=[hd])
        yield

    def phase_b(self, l, nch, ctx):
        import itertools
        kind = l % 3
        dual = (kind != 0)

        def fill(c3, c1):
            chains = []
            for j in range(2):
                g = []
                if c3 is not None:
                    g.append(self.stage3_tile_g(l, c3, j, ctx, self.sc[j]))
                if c1 is not None:
                    g.append(self.stage1_tile_g(l, c1, j, ctx, self.sc[j]))
                chains.append(itertools.chain(*g))
            return _round_robin(chains) if dual else itertools.chain(*chains)

        self.use_psum(1, own=True)
        for _ in _round_robin([self.stage1_tile_g(l, 0, j, ctx, self.sc[j]) for j in range(2)]):
            pass
        self.use_psum(1, own=dual)
        for c in range(nch):
            filler = fill(c - 1 if c > 0 else None, c + 1 if c + 1 < nch else None)
            self.attention(l, c, ctx, filler)
            for _ in filler:
                pass
        self.use_psum(1, own=True)
        for _ in _round_robin([self.stage3_tile_g(l, nch - 1, j, ctx, self.sc[j]) for j in range(2)]):
            pass

    def attention(self, l, c, ctx, filler=None):
        b, ds = self.b, self.ds
        si = c % 2
        QT, QTb = self.QTs[si], b["QT%d" % si]
        zsT, zsTb = self.zsTs[si], b["zsT%d" % si]
        kind = l % 3
        sink = (kind == 1)
        blocks = []
        if ctx:
            for j in range(2):
                kb = 2 * c + j
                blocks.append(("c", kb, None))
        else:
            if kind == 0:
                for kb in range(NLT):
                    blocks.append(("l", kb, None))
            elif kind == 1:
                for o in range(-1, 3):
                    kb = 2 * c + o
                    if 0 <= kb < NLT:
                        blocks.append(("l", kb, ("w", o + 1)))
            else:
                for o in range(-2, 4):
                    kb = 2 * c + o
                    if 0 <= kb < NLT:
                        blocks.append(("l", kb, ("n", 7 - 2 * o)))
            blocks.append(("l", NLT, None))
            blocks.append(("l", NLT + 1, None))
        import os
        if not ctx:
            blocks = blocks[:int(os.environ.get('DBG_NKB', '99'))]
        na = (not ctx) and kind == 2
        variant = 1 if (c == 0 or c == TL // NQ - 1) else 0
        if kind == 0:
            GS = 4
            Sap = [self.ps_s[0][:, :], self.ps_s[1][:, :]]
            psb = [[b["pb0"], b["pb1"]], [b["pb2"], b["pb3"]]]
        else:
            GS = 2
            Sap = [self.ps_s[1][:, 0:512], self.ps_s[1][:, 512:1024]]
            psb = [[b["pb2"]], [b["pb3"]]]
        groups = [blocks[i:i + GS] for i in range(0, len(blocks), GS)]
        units = []
        for h in range(NH if ctx else int(os.environ.get('DBG_NHD', '16'))):
            for gi, g in enumerate(groups):
                units.append((h, gi, g, gi == 0, gi == len(groups) - 1))
        P = [self.A3[:, 0:512].bitcast(BF16), self.A3[:, 512:1024].bitcast(BF16), self.P3[:, :]]
        Pb = [b["A3a"], b["A3b"], b["P3"]]
        ob = [b["ps_o0"], b["ps_o1"]]
        state = {"acc_started": {}}

        def kt_ap(kindb, kb, kv, base):
            if kindb == "c":
                return self.KTc[:, kv // 2, kb * 128:(kb + 1) * 128], self.bKTc[kb]
            return self.KT[:, kv // 2, kb * 128:(kb + 1) * 128], self.bKT[kb]

        def v_ap(kindb, kb, kv):
            if kindb == "c":
                return self.VVc[:, kb, kv, :], self.bVVc[kb]
            return self.VV[:, kb, kv, :], self.bVV[kb]

        def emit_qk(ui):
            h, gi, g, first, lastg = units[ui]
            kv, base = PERM[h] // 4, (h % 2) * 64
            S, Sb = Sap[ui % 2], psb[ui % 2]
            if na and gi == 0:
                ti = h % 2
                self.dma(self.sp, self.tbl[ti][:, 0:960], self.nab.ap()[variant, PERM[h]], "tbl%d" % ti, reads=[b["nab"]], writes=[b["tbl%d" % ti]])
            for j, (kb_kind, kb, extra) in enumerate(g):
                kt, ktb = kt_ap(kb_kind, kb, kv, base)
                out = S[:, j * NQ:(j + 1) * NQ]
                lastmm = (j == len(g) - 1)
                self.run(self.pe, lambda e, out=out, kt=kt, h=h, base=base, extra=extra: e.matmul(
                    out, lhsT=kt, rhs=QT[:, h, :], start=True, stop=(extra is None)),
                    reads=[ktb, QTb], writes=Sb, inc=(lastmm and extra is None))
                if extra is not None:
                    if extra[0] == "w":
                        ex, exb = self.wmask[:, extra[1], :], b["tbl0"]
                    else:
                        ti = h % 2
                        ex, exb = self.tbl[ti][:, extra[1] * 64:(extra[1] + 4) * 64], b["tbl%d" % ti]
                    self.run(self.pe, lambda e, out=out, ex=ex: e.matmul(out, lhsT=self.ident[:], rhs=ex, start=False, stop=True),
                             reads=[exb, b["consts"]], writes=Sb, inc=lastmm)

        def emit_exp(ui):
            h, gi, g, first, lastg = units[ui]
            n = len(g) * NQ
            S, Sb = Sap[ui % 2], psb[ui % 2]
            self.run(self.act, lambda e, S=S, n=n, ui=ui: e.activation(out=P[ui % 3][:, 0:n], in_=S[:, 0:n], func=AF.Exp),
                     reads=Sb, writes=[Pb[ui % 3]])

        def emit_pv(ui):
            h, gi, g, first, lastg = units[ui]
            kv = PERM[h] // 4
            O = self.ps_oo[h % 2][:, 0:NQ]
            Ob = ob[h % 2]
            for j, (kb_kind, kb, extra) in enumerate(g):
                va, vb = v_ap(kb_kind, kb, kv)
                rhs = P[ui % 3][:, j * NQ:(j + 1) * NQ]
                st_ = first and j == 0
                sp_ = lastg and j == len(g) - 1
                self.run(self.pe, lambda e, O=O, va=va, rhs=rhs, st_=st_, sp_=sp_: e.matmul(O[0:64, :], lhsT=va, rhs=rhs, start=st_, stop=sp_),
                         reads=[vb, Pb[ui % 3]], writes=[Ob], inc=False)
                self.run(self.pe, lambda e, O=O, rhs=rhs, st_=st_, sp_=sp_: e.matmul(O[64:128, :], lhsT=self.ones64[:, :], rhs=rhs, start=st_, stop=sp_),
                         reads=[b["consts"], Pb[ui % 3]], writes=[Ob], inc=(j == len(g) - 1))

        def emit_post(h):
            base = (h % 2) * 64
            O = self.ps_oo[h % 2][:, 0:NQ]
            Ob = ob[h % 2]
            lnd = self.A4[64:128, 0:NQ]
            rden = self.A4[64:128, NQ:2 * NQ]
            tt = self.A4[base:base + 64, (2 + h % 2) * NQ:(3 + h % 2) * NQ]
            ttb = b["A4c"] if h % 2 == 0 else b["A4d"]
            if kind == 1:
                if sink:
                    self.run(self.act, lambda e: e.activation(out=lnd, in_=O[64:128, :], func=AF.Ln, bias=self.esink[64:128, PERM[h]:PERM[h] + 1]),
                             reads=[Ob, b["esink"]], writes=[b["A4a"]])
                else:
                    self.run(self.act, lambda e: e.activation(out=lnd, in_=O[64:128, :], func=AF.Ln), reads=[Ob], writes=[b["A4a"]])
                self.run(self.act, lambda e: e.activation(out=rden, in_=lnd, func=AF.Exp, scale=-1.0), reads=[b["A4a"]], writes=[b["A4b"]])
            elif sink:
                self.run(self.dve, lambda e: e.tensor_scalar(lnd, O[64:128, :], self.esink[64:128, PERM[h]:PERM[h] + 1], None, ALU.add),
                         reads=[Ob, b["esink"]], writes=[b["A4a"]])
                self.run(self.dve, lambda e: e.reciprocal(out=rden, in_=lnd), reads=[b["A4a"]], writes=[b["A4b"]])
            else:
                self.run(self.dve, lambda e: e.reciprocal(out=rden, in_=O[64:128, :]), reads=[Ob], writes=[b["A4b"]])
            self.run(self.dve, lambda e: e.scalar_tensor_tensor(out=tt, in0=O[0:64, :], scalar=0.5, in1=rden, op0=ALU.mult, op1=ALU.mult),
                     reads=[Ob, b["A4b"]], writes=[ttb])
            self.run(self.pool, lambda e: e.tensor_tensor(out=zsT[base:base + 64, h // 2, :], in0=tt,
                                                          in1=zsT[base:base + 64, h // 2, :], op=ALU.mult),
                     reads=[ttb, zsTb], writes=[zsTb])

        nU = len(units)
        if os.environ.get('DBG_NOPIPE'):
            for ui in range(nU):
                emit_qk(ui)
                emit_exp(ui)
                emit_pv(ui)
                if units[ui][4]:
                    emit_post(units[ui][0])
            return
        emit_qk(0)
        if nU > 1:
            emit_qk(1)
        pend = None
        import math
        k_fill = max(1, math.ceil(90.0 / nU))
        for ui in range(nU):
            if filler is not None and ui >= 2:
                for _ in range(k_fill):
                    if next(filler, "done") == "done":
                        filler = None
                        break
            emit_exp(ui)
            if pend is not None:
                emit_post(pend)
                pend = None
            if ui + 2 < nU:
                emit_qk(ui + 2)
            emit_pv(ui)
            if units[ui][4]:
                pend = units[ui][0]
        emit_post(pend)


_CACHE = {}


def _constants():
    t = np.arange(TL, dtype=np.int64)
    row = (t // GRID_W).astype(np.float32)
    col = (t % GRID_W).astype(np.float32)
    inv = (np.float32(10000.0) ** (-np.arange(16, dtype=np.float32) / np.float32(16))).astype(np.float32)
    ang = np.concatenate([row[:, None] * inv, col[:, None] * inv], axis=-1).astype(np.float32)
    cos = np.cos(ang).astype(np.float32).reshape(NLT, 128, 32).transpose(1, 0, 2).copy()
    sin = np.sin(ang).astype(np.float32).reshape(NLT, 128, 32).transpose(1, 0, 2).copy()
    ident = np.eye(128, dtype=np.float32)
    wm = np.full((4, 128, NQ), NEG, dtype=np.float32)
    kk = np.arange(128)[:, None]
    qq = np.arange(128)[None, :]
    for oi, o in enumerate(range(-1, 3)):
        for j in range(2):
            d = o - j
            if d == 0:
                m = np.zeros((128, 128), np.float32)
            elif d == -1:
                m = np.where(qq <= kk, 0.0, NEG).astype(np.float32)
            elif d == 1:
                m = np.where(kk <= qq, 0.0, NEG).astype(np.float32)
            else:
                continue
            wm[oi, :, j * 128:(j + 1) * 128] = m
    cm = np.zeros((2, 128, 15, 64), dtype=np.float32)
    cc = np.arange(64)
    cs = np.clip(cc - 8, 0, 48)
    for half in range(2):
        for kc in range(64):
            p = half * 64 + kc
            colok = (kc >= cs) & (kc < cs + 16)
            for s in range(15):
                delta = (7 - s) if half == 0 else (8 - s)
                for v in range(2):
                    rowok = (-4 <= delta <= 3) if v == 0 else (-7 <= delta <= 7)
                    cm[v, p, s, :] = np.where(colok & rowok, 0.0, NEG)
    return cos, sin, ident, wm, cm.reshape(2, 128, 15 * 64)


def _tsrc(na_rel_bias):
    tab = np.asarray(na_rel_bias, dtype=np.float32)[0]
    out = np.zeros((2, NH, 15, 128), dtype=np.float32)
    rev = tab[:, :, ::-1]
    for s in range(15):
        dr0 = 14 - s
        out[0, :, s, 48:79] = rev[:, dr0, :]
        dr1 = 15 - s
        if dr1 <= 14:
            out[1, :, s, 48:79] = rev[:, dr1, :]
    return out


def _perm_w_in(w):
    idx = np.concatenate([np.arange(64) + 64 * h for h in PERM])
    cols = np.concatenate([idx, np.arange(1024, 1536), 1536 + idx])
    return np.ascontiguousarray(w[:, :, cols])


def _perm_w_out(w):
    idx = np.concatenate([np.arange(64) + 64 * h for h in PERM])
    return np.ascontiguousarray(w[:, idx, :])


def _get_prog(n_layers=NL):
    if n_layers not in _CACHE:
        p = Prog(n_layers)
        _CACHE[n_layers] = p.build()
    return _CACHE[n_layers]


def make_in_maps(inputs, n_cores=8):
    f = lambda a: np.ascontiguousarray(np.asarray(a, dtype=np.float32))
    cos, sin, ident, wm, cm = _constants()
    tsrc = _tsrc(inputs["na_rel_bias"])
    shared = {
        "w_mod": f(inputs["w_mod"]), "b_mod": f(inputs["b_mod"]), "norm_pre": f(inputs["norm_pre"]),
        "norm_post": f(inputs["norm_post"]), "w_in": _perm_w_in(f(inputs["w_in"])), "q_norm": f(inputs["q_norm"]),
        "k_norm": f(inputs["k_norm"]), "w_out": _perm_w_out(f(inputs["w_out"])), "sink": f(inputs["sink_logit"]).reshape(1, NH),
        "tsrc": tsrc, "cos_t": cos, "sin_t": sin, "ident_f": ident, "wmask_f": wm, "cmask_f": cm,
    }
    xs, xp = f(inputs["x_sample"]), f(inputs["x_prompt"])
    ck, cv = f(inputs["cache_k"]), f(inputs["cache_v"])
    c, cctx = f(inputs["c"]), f(inputs["c_ctx"])
    maps = []
    for i in range(n_cores):
        m = dict(shared)
        m["xs"] = xs[i]
        m["xp"] = xp[2 * i:2 * i + 2].reshape(TCX, D)
        m["ck"] = ck[i].reshape(NL, 256, 256)
        m["cv"] = cv[i].reshape(NL, 256, 256)
        m["cond"] = np.stack([c[i], cctx], axis=0)
        maps.append(m)
    return maps


def kernel(x_prompt, x_sample, cache_k, cache_v, c, c_ctx, w_mod, b_mod, norm_pre, norm_post,
           w_in, q_norm, k_norm, w_out, sink_logit, na_rel_bias):
    inputs = dict(x_prompt=x_prompt, x_sample=x_sample, cache_k=cache_k, cache_v=cache_v, c=c, c_ctx=c_ctx,
                  w_mod=w_mod, b_mod=b_mod, norm_pre=norm_pre, norm_post=norm_post, w_in=w_in, q_norm=q_norm,
                  k_norm=k_norm, w_out=w_out, sink_logit=sink_logit, na_rel_bias=na_rel_bias)
    nc = _get_prog(NL)
    maps = make_in_maps(inputs, 8)
    res = run_bass_kernel_spmd(nc, maps, core_ids=list(range(8)))
    r = res.results
    y_prompt = np.concatenate([r[i]["yp"].reshape(2, 256, D) for i in range(8)], axis=0).astype(np.float32)
    y_sample = np.stack([r[i]["ys"] for i in range(8)], axis=0).astype(np.float32)
    nk = np.concatenate([r[i]["nk"].reshape(2, NL, 256, NKV, HD) for i in range(8)], axis=0).astype(np.float32)
    nv = np.concatenate([r[i]["nv"].reshape(2, NL, 256, NKV, HD) for i in range(8)], axis=0).astype(np.float32)
    return (y_prompt, y_sample, nk, nv)
```

```python
import numpy as np
from contextlib import ExitStack
import concourse.bass as bass
import concourse.mybir as mybir
from concourse.bass_utils import run_bass_kernel_spmd

F32 = mybir.dt.float32
BF16 = mybir.dt.bfloat16
AF = mybir.ActivationFunctionType
ALU = mybir.AluOpType
AX = mybir.AxisListType

D = 1024
NL = 4
NH = 16
NKV = 4
HD = 64
TL = 4096
TCX = 512
NQ = 256
NLT = TL // 128
NKB = NLT + 2
EPS = 1e-6
NEG = -30000.0
GRID_W = 64
PERM = [0, 4, 1, 5, 2, 6, 3, 7, 8, 12, 9, 13, 10, 14, 11, 15]


class Sem:
    def __init__(self, h, name):
        self.h = h
        self.cnt = 0
        self.name = name


class Buf:
    def __init__(self, name):
        self.name = name
        self.w = None
        self.r = {}


class Queue:
    def __init__(self, name, sem):
        self.name = name
        self.sem = sem
        self.ops = []
        self.seen = {}
        self.pending = False

    def wait(self, toks):
        for t in toks:
            if t is None:
                continue
            sem, val = t
            if sem is self.sem and (val > sem.cnt or self.name == "pe"):
                continue
            if self.seen.get(sem, 0) >= val:
                continue
            self.seen[sem] = val
            self.ops.append(("wait", sem, val))


def _round_robin(gens):
    gens = list(gens)
    while gens:
        for g in list(gens):
            try:
                yield next(g)
            except StopIteration:
                gens.remove(g)


def _mark(reads, writes, tok):
    for b in reads:
        if b.r.get(tok[0], 0) < tok[1]:
            b.r[tok[0]] = tok[1]
    for b in writes:
        b.w = tok
        b.r = {}


def _deps(reads, writes):
    deps = []
    for b in reads:
        deps.append(b.w)
    for b in writes:
        deps.append(b.w)
        deps.extend(b.r.items())
    return deps


class Prog:
    def __init__(self, n_layers=NL):
        self.n_layers = n_layers
        self.nc = bass.Bass("TRN2", target_bir_lowering=False)
        self.es = ExitStack()
        self.sems = []
        self.dma_sems = []

    def sem(self, name):
        s = Sem(self.es.enter_context(self.nc.semaphore(name)), name)
        self.sems.append(s)
        return s

    def dsem(self, name):
        s = self.sem(name)
        self.dma_sems.append(s)
        return s

    def sb(self, name, shape, dt):
        return self.es.enter_context(self.nc.sbuf_tensor(name, shape, dt))

    def psum(self, name, shape, dt):
        return self.es.enter_context(self.nc.psum_tensor(name, shape, dt))

    def run(self, q, fn, reads=(), writes=(), inc=True):
        q.wait(_deps(reads, writes))
        if inc:
            q.sem.cnt += 1
            tok = (q.sem, q.sem.cnt)
            q.ops.append(("op", fn, True))
            q.pending = False
        else:
            tok = (q.sem, q.sem.cnt + 1)
            q.ops.append(("op", fn, False))
            q.pending = True
        _mark(reads, writes, tok)
        return tok

    def dma(self, q, out, in_, sem, reads=(), writes=(), **kw):
        if isinstance(sem, str):
            if sem not in self.ds:
                self.ds[sem] = self.dsem("d_" + sem)
            sem = self.ds[sem]
        q.wait(_deps(reads, writes))
        sem.cnt += 16
        tok = (sem, sem.cnt)
        q.ops.append(("dma", out, in_, sem, kw))
        _mark(reads, writes, tok)
        return tok

    def barrier(self):
        toks = []
        for q in self.cq:
            assert not q.pending, q.name
            toks.append((q.sem, q.sem.cnt))
        for s in self.dma_sems:
            toks.append((s, s.cnt))
        for q in self.allq:
            q.wait(toks)

    def emit(self, q, e):
        for op in q.ops:
            if op[0] == "wait":
                e.wait_ge(op[1].h, op[2])
            elif op[0] == "op":
                ins = op[1](e)
                if op[2]:
                    ins.then_inc(q.sem.h, 1)
            else:
                e.dma_start(out=op[1], in_=op[2], **op[4]).then_inc(op[3].h, 16)

    def build(self):
        nc = self.nc
        dram = lambda n, s, k="ExternalInput", dt=F32: nc.dram_tensor(n, list(s), dt, kind=k)
        self.xs = dram("xs", (TL, D))
        self.xp = dram("xp", (TCX, D))
        self.ck = dram("ck", (NL, 256, 256))
        self.cv = dram("cv", (NL, 256, 256))
        self.cond = dram("cond", (2, D))
        self.w_mod = dram("w_mod", (NL, D, 3 * D))
        self.b_mod = dram("b_mod", (NL, 3 * D))
        self.norm_pre = dram("norm_pre", (NL, D))
        self.norm_post = dram("norm_post", (NL, D))
        self.w_in = dram("w_in", (NL, D, 2560))
        self.q_norm = dram("q_norm", (NL, HD))
        self.k_norm = dram("k_norm", (NL, HD))
        self.w_out = dram("w_out", (NL, D, D))
        self.sink = dram("sink", (1, NH))
        self.tsrc = dram("tsrc", (2, NH, 15, 128))
        self.cos_d = dram("cos_t", (128, NLT, 32))
        self.sin_d = dram("sin_t", (128, NLT, 32))
        self.ident_d = dram("ident_f", (128, 128))
        self.wmask_d = dram("wmask_f", (4, 128, NQ))
        self.cmask_d = dram("cmask_f", (2, 128, 15 * 64))
        self.ys = dram("ys", (TL, D), "ExternalOutput")
        self.yp = dram("yp", (TCX, D), "ExternalOutput")
        self.nk = dram("nk", (2, NL, 256, 256), "ExternalOutput")
        self.nv = dram("nv", (2, NL, 256, 256), "ExternalOutput")
        self.hs_d = dram("hs_scr", (TL, D), "Internal")
        self.hp_d = dram("hp_scr", (TCX, D), "Internal")
        self.modrow = dram("modrow", (NL, 2, 3 * D), "Internal")
        self.tzD = dram("tz_scr", (128, NH, 15 * 64), "Internal")
        self.nab = dram("nab_scr", (2, NH, 128, 15 * 64), "Internal", BF16)

        self.pe = Queue("pe", self.sem("s_pe"))
        self.act = Queue("act", self.sem("s_act"))
        self.dve = Queue("dve", self.sem("s_dve"))
        self.pool = Queue("pool", self.sem("s_pool"))
        self.sp = Queue("sp", self.sem("s_sp"))
        self.cq = [self.pe, self.act, self.dve, self.pool]
        self.allq = [self.pe, self.act, self.dve, self.pool, self.sp]

        sb, ps = self.sb, self.psum
        self.win = sb("win", [128, 8, 2560], BF16)
        self.wout = sb("wout", [128, 8, 1024], BF16)
        self.KT = sb("KT", [128, 2, NKB * 128], BF16)
        self.VV = sb("VV", [128, NKB, NKV, HD], BF16)
        self.KTc = self.KT
        self.VVc = self.VV
        self.mods = sb("mods", [128, 3 * D], F32)
        self.cosT = sb("cosT", [128, NLT, 32], F32)
        self.sinT = sb("sinT", [128, NLT, 32], F32)
        self.ident = sb("ident", [128, 128], BF16)
        self.ones64 = sb("ones64", [128, 64], BF16)
        self.A1 = sb("A1", [128, 1024], F32)
        self.A2 = sb("A2", [128, 1024], F32)
        self.A3 = sb("A3", [128, 1024], F32)
        self.A4 = sb("A4", [128, 1024], F32)
        self.B1 = sb("B1", [128, 1024], F32)
        self.B2 = sb("B2", [128, 1024], F32)
        self.hB = [sb("hB0", [128, 1024], F32), sb("hB1", [128, 1024], F32)]
        self.u = sb("u", [128, 1024], BF16)
        self.u1 = sb("u1", [128, 1024], BF16)
        self.P3 = sb("P3", [128, 1024], BF16)
        self.uTg = sb("uTg", [128, 8, NQ], BF16)
        self.QTs = [sb("QT0", [128, NH, NQ], BF16), sb("QT1", [128, NH, NQ], BF16)]
        self.zsTs = [sb("zsT0", [128, 8, NQ], BF16), sb("zsT1", [128, 8, NQ], BF16)]
        self.hR = [sb("hR0", [128, 1024], F32), sb("hR1", [128, 1024], F32)]
        self.R3 = sb("R3", [128, 1024], F32)
        self.R4 = sb("R4", [128, 1024], F32)
        self.S3 = sb("S3", [128, 1024], F32)
        self.S4 = sb("S4", [128, 1024], F32)
        self.kd = sb("kd", [128, NKV, HD], BF16)
        self.kd1 = sb("kd1", [128, NKV, HD], BF16)
        self.kv32 = self.R3[:, :].rearrange("p (i c) -> p i c", i=2)
        self.tbl = [sb("tbl0", [128, 1024], BF16), sb("tbl1", [128, 1024], BF16)]
        self.wmask = self.tbl[0][:, :].rearrange("p (o q) -> p o q", q=NQ)
        self.stat = sb("stat", [128, 256], F32)
        self.gq = sb("gq", [128, HD], F32)
        self.gk = sb("gk", [128, HD], F32)
        self.esink = sb("esink", [128, NH], F32)
        self.mhalf = sb("mhalf", [128, 16], F32)
        self.condT = sb("condT", [128, 2, 8], F32)
        self.scT = sb("scT", [128, 8, 2], F32)
        self.ps_s = [ps("ps_s0", [128, 1024], F32), ps("ps_s1", [128, 1024], F32)]
        self.ps_oo = [ps("ps_o0", [128, 512], F32), ps("ps_o1", [128, 512], F32)]
        self.ps_p = ps("ps_p", [128, 1024], F32)
        self.ps_t = self.ps_p[:, 512:1024].bitcast(BF16)

        B = Buf
        self.b = {n: B(n) for n in [
            "win", "wout", "mods", "consts", "A1", "A2", "A3a", "A3b", "A4a", "A4b", "A4c", "A4d",
            "hB0", "hB1", "hR0", "hR1", "R3a", "R3b", "R4a", "R4b", "S3a", "S3b", "S4a", "S4b", "B1", "B2", "u", "u1", "kd1", "P3", "uTg0", "uTg1", "pb0", "pb1", "pb2", "pb3", "st_ln1", "st_q1", "st_y1", "QT0", "QT1", "zsT0", "zsT1", "kd", "tbl0", "tbl1", "gq", "gk",
            "esink", "condT", "scT", "ps_s0", "ps_s1", "ps_o0", "ps_o1", "ps_t", "ps_p0", "ps_p1",
            "st_ln", "st_q", "st_k", "st_y", "modrow", "tzD", "nab"]}
        self.b["ps_t"] = self.b["ps_p1"]
        self.mk_scratch_sets()
        self.b["kv32a"] = self.b["R3a"]
        self.b["kv32b"] = self.b["R3b"]
        self.bKT = [B("KT%d" % i) for i in range(NKB)]
        self.bVV = [B("VV%d" % i) for i in range(NKB)]
        self.bKTc = self.bKT
        self.bVVc = self.bVV
        self.bhs = [B("hs%d" % i) for i in range(NLT)]
        self.bhp = [B("hp%d" % i) for i in range(4)]
        self.bout = B("outkv")
        self.ds = {}

        self.prologue()
        for l in range(self.n_layers):
            self.layer(l)
        self.barrier()

        blk = self.es.enter_context(nc.Block())

        @blk.tensor
        def _(e):
            self.emit(self.pe, e)

        @blk.scalar
        def _(e):
            self.emit(self.act, e)

        @blk.vector
        def _(e):
            self.emit(self.dve, e)

        @blk.gpsimd
        def _(e):
            self.emit(self.pool, e)

        @blk.sync
        def _(e):
            self.emit(self.sp, e)

        self.es.close()
        return nc

    def mk_scratch_sets(self):
        b = self.b

        class SC:
            pass
        self.sc = []
        for k in range(2):
            sc = SC()
            sc.k = k
            sc.A1, sc.bA1 = (self.A1, b["A1"]) if k == 0 else (self.B1, b["B1"])
            sc.A2, sc.bA2 = (self.A2, b["A2"]) if k == 0 else (self.B2, b["B2"])
            sc.kA2 = "A2" if k == 0 else "B2"
            sc.R3, sc.bR3a, sc.bR3b = (self.R3, b["R3a"], b["R3b"]) if k == 0 else (self.S3, b["S3a"], b["S3b"])
            sc.kR3 = "R3a" if k == 0 else "S3a"
            sc.R4, sc.bR4a, sc.bR4b = (self.R4, b["R4a"], b["R4b"]) if k == 0 else (self.S4, b["S4a"], b["S4b"])
            sc.u, sc.bu = (self.u, b["u"]) if k == 0 else (self.u1, b["u1"])
            sc.kd, sc.bkd = (self.kd, b["kd"]) if k == 0 else (self.kd1, b["kd1"])
            sc.hB, sc.bhB, sc.khB = self.hB[k], b["hB%d" % k], "hB%d" % k
            sc.hR, sc.bhR, sc.khR = self.hR[k], b["hR%d" % k], "hR%d" % k
            sc.uT, sc.buT = self.uTg[:, :, k * 128:(k + 1) * 128], b["uTg%d" % k]
            sc.so = 128 * k
            sc.bst_ln = b["st_ln"] if k == 0 else b["st_ln1"]
            sc.bst_q = b["st_q"] if k == 0 else b["st_q1"]
            sc.bst_y = b["st_y"] if k == 0 else b["st_y1"]
            self.sc.append(sc)
        self.use_psum(0, own=True)
        self.use_psum(1, own=True)

    def use_psum(self, k, own):
        b = self.b
        sc = self.sc[k]
        if k == 0 or not own:
            sc.ps_p, sc.bps = self.ps_p, [b["ps_p0"], b["ps_p1"]]
        else:
            sc.ps_p, sc.bps = self.ps_s[0], [b["pb0"], b["pb1"]]
        sc.ps_t, sc.bpt = sc.ps_p[:, 512:1024].bitcast(BF16), sc.bps[1]

    def prologue(self):
        b, ds = self.b, self.ds
        c = [b["consts"]]
        self.dma(self.sp, self.cosT[:], self.cos_d.ap()[:, :, :], "consts", writes=c)
        self.dma(self.sp, self.sinT[:], self.sin_d.ap()[:, :, :], "consts", writes=c)
        self.dma(self.pool, self.ident[:], self.ident_d.ap()[:, :], "sw_consts", writes=c)
        self.run(self.pool, lambda e: e.memset(self.ones64[:], 1.0), writes=c)
        self.run(self.pool, lambda e: e.memset(self.mhalf[:], -0.5), writes=c)
        for i_ in range(2):
            self.run(self.pool, lambda e, i_=i_: e.memset(self.QTs[i_][:], 0.0), writes=[b["QT%d" % i_]])
        self.dma(self.sp, self.esink[:], self.sink.ap()[0:1, :].partition_broadcast(128), "esink", writes=[b["esink"]])
        self.run(self.act, lambda e: e.activation(out=self.esink[:], in_=self.esink[:], func=AF.Exp),
                 reads=[b["esink"]], writes=[b["esink"]])
        for j in range(2):
            self.dma(self.sp, self.condT[:, j, :], bass.AP(self.cond, j * D, [[1, 128], [128, 8]]), "condT",
                     writes=[b["condT"]], allow_slow_non_contiguous=True)
        self.run(self.act, lambda e: e.activation(out=self.scT[:, :, :].rearrange("p k j -> p j k"), in_=self.condT[:, :, :], func=AF.Silu),
                 reads=[b["condT"]], writes=[b["scT"]])
        self.load_weights(0)
        if self.n_layers > 2:
            self.na_tables()
        self.barrier()

    def na_tables(self):
        b, ds = self.b, self.ds
        for half in range(2):
            for kc in range(64):
                p = half * 64 + kc
                self.dma(self.sp, self.tzD.ap()[p:p + 1, :, :].rearrange("p h (s c) -> p (h s) c", c=64),
                         bass.AP(self.tsrc, half * NH * 15 * 128 + 63 - kc, [[0, 1], [128, NH * 15], [1, 64]]),
                         "tzD")
        b["tzD"].w = (self.ds["tzD"], self.ds["tzD"].cnt)
        b["tzD"].r = {}
        cm = [self.A2, self.A3]
        self.dma(self.sp, self.A2[:, 0:960], self.cmask_d.ap()[0], "A2", writes=[b["A2"]])
        self.dma(self.sp, self.A3[:, 0:960], self.cmask_d.ap()[1], "A3a", writes=[b["A3a"], b["A3b"]])
        cmb = [[b["A2"]], [b["A3a"], b["A3b"]]]
        ins = [(self.A1, b["A1"], "A1"), (self.hB[0], b["hB0"], "hB0")]
        obs = [(self.A4[:, 0:480].bitcast(BF16), [b["A4a"], b["A4b"]], "A4a"),
               (self.A4[:, 512:992].bitcast(BF16), [b["A4c"], b["A4d"]], "A4c")]
        for h in range(NH):
            it, ib, ik = ins[h % 2]
            self.dma(self.sp, it[:, 0:960], self.tzD.ap()[:, h, :], ik, reads=[b["tzD"]], writes=[ib])
            for v in range(2):
                ot, obufs, ok = obs[v]
                self.run(self.dve, lambda e, v=v, it=it, ot=ot: e.tensor_tensor(out=ot, in0=it[:, 0:960], in1=cm[v][:, 0:960], op=ALU.add),
                         reads=[ib] + cmb[v], writes=obufs)
                self.dma(self.sp, self.nab.ap()[v, h], ot, ok, reads=obufs)

    def load_weights(self, l):
        b = self.b
        for kc in range(8):
            self.dma(self.pool, self.win[:, kc, :], self.w_in.ap()[l, kc * 128:(kc + 1) * 128, :], "win", writes=[b["win"]])
        for kc in range(8):
            self.dma(self.pool, self.wout[:, kc, :], self.w_out.ap()[l, kc * 128:(kc + 1) * 128, :], "wout", writes=[b["wout"]])

    def layer(self, l):
        b, ds = self.b, self.ds
        kind = l % 3
        if l > 0:
            self.load_weights(l)
        self.dma(self.sp, self.gq[:], self.q_norm.ap()[l:l + 1, :].partition_broadcast(128), "gq", writes=[b["gq"]])
        self.dma(self.sp, self.gk[:], self.k_norm.ap()[l:l + 1, :].partition_broadcast(128), "gk", writes=[b["gk"]])
        self.run(self.dve, lambda e: e.tensor_scalar(self.gq[:], self.gq[:], HD ** -0.5, None, ALU.mult),
                 reads=[b["gq"]], writes=[b["gq"]])
        import os
        stg = int(os.environ.get("DBG_STAGE", "99"))
        nch = int(os.environ.get("DBG_NCH", str(TL // NQ)))
        if stg < 2:
            return
        if kind == 1:
            self.dma(self.pool, self.wmask, self.wmask_d.ap().rearrange("o p q -> p o q"), "sw_tbl0", writes=[b["tbl0"]])
        self.modulation_rows(l)
        if stg < 3:
            return
        self.load_mods(l, 1)
        self.phase_a(l, 4, ctx=True)
        if stg < 4:
            return
        self.phase_b(l, 2, ctx=True)
        if stg < 5:
            return
        self.load_mods(l, 0)
        self.cached_kv(l)
        self.phase_a(l, NLT, ctx=False)
        if stg < 6:
            return
        self.phase_b(l, nch, ctx=False)

    def modulation_rows(self, l):
        b, ds = self.b, self.ds
        stg = [self.A1, self.A2, self.hB[0], self.hB[1]]
        wm = [t_[:, :].rearrange("p (k n) -> p k n", n=128) for t_ in stg]
        wmb = [b["A1"], b["A2"], b["hB0"], b["hB1"]]
        wmk = ["A1", "A2", "hB0", "hB1"]
        NCH = 3 * D // 128
        for ch in range(NCH):
            i = ch % 4
            self.dma(self.sp, wm[i], self.w_mod.ap()[l, :, ch * 128:(ch + 1) * 128].rearrange("(k p) n -> p k n", p=128),
                     wmk[i], writes=[wmb[i]])
            pv = self.ps_p[0:2, (ch % 4) * 128:(ch % 4) * 128 + 128]
            pb = b["ps_p0"]
            for kc in range(8):
                self.run(self.pe, lambda e, kc=kc, i=i, pv=pv: e.matmul(pv, lhsT=self.scT[:, kc, :], rhs=wm[i][:, kc, :],
                                                                   start=(kc == 0), stop=(kc == 7)),
                         reads=[wmb[i], b["scT"]], writes=[pb], inc=(kc == 7))
            if ch % 4 == 3:
                c0 = (ch - 3) * 128
                bm = self.A3[0:2, 0:512]
                mo = self.A3[0:2, 512:1024]
                self.dma(self.sp, bm, self.b_mod.ap()[l:l + 1, c0:c0 + 512].partition_broadcast(2), "A3a", writes=[b["A3a"]])
                self.run(self.dve, lambda e, bm=bm, mo=mo: e.tensor_tensor(out=mo, in0=self.ps_p[0:2, 0:512], in1=bm, op=ALU.add),
                         reads=[pb, b["A3a"]], writes=[b["A3b"]])
                self.dma(self.sp, self.modrow.ap()[l, :, c0:c0 + 512], mo, "A3b", reads=[b["A3b"]], writes=[b["modrow"]])

    def load_mods(self, l, j):
        b, ds = self.b, self.ds
        self.dma(self.sp, self.mods[:], self.modrow.ap()[l, j:j + 1, :].partition_broadcast(128), "mods",
                 reads=[b["modrow"]], writes=[b["mods"]])
        self.dma(self.sp, self.A1[:], self.norm_pre.ap()[l:l + 1, :].partition_broadcast(128), "A1", writes=[b["A1"]])
        self.dma(self.sp, self.A2[:], self.norm_post.ap()[l:l + 1, :].partition_broadcast(128), "A2", writes=[b["A2"]])
        self.run(self.dve, lambda e: e.scalar_tensor_tensor(out=self.mods[:, D:2 * D], in0=self.mods[:, D:2 * D], scalar=1.0,
                                                            in1=self.A1[:], op0=ALU.add, op1=ALU.mult),
                 reads=[b["mods"], b["A1"]], writes=[b["mods"]])
        self.run(self.dve, lambda e: e.tensor_tensor(out=self.mods[:, 2 * D:3 * D], in0=self.mods[:, 2 * D:3 * D], in1=self.A2[:], op=ALU.mult),
                 reads=[b["mods"], b["A2"]], writes=[b["mods"]])

    def rstd_small(self, ss, tmp, out, n, inv_n, bufs):
        self.run(self.pool, lambda e: e.tensor_scalar(tmp, ss, inv_n, EPS, ALU.mult, ALU.add), reads=bufs, writes=bufs)
        self.run(self.pool, lambda e: e.tensor_tensor(out=out, in0=tmp, in1=self.mhalf[:, 0:n], op=ALU.pow),
                 reads=bufs + [self.b["consts"]], writes=bufs)

    def ln_u_g(self, sc, hbuf, hB):
        ps_p, ps_t, bps, bpt = sc.ps_p, sc.ps_t, sc.bps, sc.bpt
        b = self.b
        st = self.stat
        o = sc.so
        sb_ = [sc.bst_ln]
        self.run(self.act, lambda e: e.activation(out=sc.A1[:], in_=hbuf[:], func=AF.Square, accum_out=st[:, o:o + 1]),
                 reads=[hB], writes=[sc.bA1, sc.bst_ln])
        yield
        self.rstd_small(st[:, o:o + 1], st[:, o + 1:o + 2], st[:, o + 2:o + 3], 1, 1.0 / D, sb_)
        yield
        self.run(self.dve, lambda e: e.scalar_tensor_tensor(out=sc.A2[:], in0=hbuf[:], scalar=st[:, o + 2:o + 3], in1=self.mods[:, D:2 * D],
                                                            op0=ALU.mult, op1=ALU.mult),
                 reads=[hB, sc.bst_ln, b["mods"]], writes=[sc.bA2])
        yield
        self.run(self.dve, lambda e: e.tensor_tensor(out=sc.u[:], in0=sc.A2[:], in1=self.mods[:, 0:D], op=ALU.add),
                 reads=[sc.bA2, b["mods"]], writes=[sc.bu])
        yield
        for kc in range(8):
            self.run(self.pe, lambda e, kc=kc: e.transpose(ps_t[:, kc * 128:(kc + 1) * 128], sc.u[:, kc * 128:(kc + 1) * 128], self.ident[:]),
                     reads=[sc.bu, b["consts"]], writes=[bpt], inc=(kc == 7))
        yield
        self.run(self.dve, lambda e: e.tensor_copy(out=sc.uT, in_=ps_t[:, :].rearrange("p (k t) -> p k t", t=128)),
                 reads=[bpt], writes=[sc.buT])
        yield

    def prep_heads_g(self, sc, src, nh, g, rope_tile, sbuf, outs):
        b = self.b
        RT, rba, rbb = sc.R3, sc.bR3a, sc.bR3b
        W = nh * HD
        st = self.stat
        o = sc.so
        if nh == NH:
            ss, ln, rs = st[:, o + 8:o + 24], st[:, o + 24:o + 40], st[:, o + 40:o + 56]
        else:
            ss, ln, rs = st[:, o + 60:o + 64], st[:, o + 64:o + 68], st[:, o + 68:o + 72]
        sq = sc.A1[:, 0:W]
        xn = sc.A2[:, 0:W]
        v3 = lambda ap: ap.rearrange("p (h d) -> p h d", d=HD)
        self.run(self.act, lambda e: e.activation(out=sq, in_=src, func=AF.Square), reads=sbuf, writes=[sc.bA1])
        yield
        self.run(self.dve, lambda e: e.tensor_reduce(out=ss, in_=v3(sq), axis=AX.X, op=ALU.add), reads=[sc.bA1], writes=[sc.bst_q])
        yield
        self.rstd_small(ss, ln, rs, nh, 1.0 / HD, [sc.bst_q])
        yield
        self.run(self.dve, lambda e: e.tensor_tensor(out=v3(xn), in0=v3(src), in1=g[:, :].unsqueeze(1).to_broadcast([128, nh, HD]), op=ALU.mult),
                 reads=sbuf + [b["gq"], b["gk"]], writes=[sc.bA2])
        yield
        fin, finb = xn, sc.bA2
        if rope_tile is not None:
            cosb = self.cosT[:, rope_tile, :].unsqueeze(1).to_broadcast([128, nh, 32])
            sinb = self.sinT[:, rope_tile, :].unsqueeze(1).to_broadcast([128, nh, 32])
            H = nh * 32
            ra = RT[:, 0:H].rearrange("p (h d) -> p h d", d=32)
            rb = RT[:, 512:512 + H].rearrange("p (h d) -> p h d", d=32)
            x1, x2 = v3(xn)[:, :, 0:32], v3(xn)[:, :, 32:64]
            xr = sc.A1[:, 0:W]
            o1, o2 = v3(xr)[:, :, 0:32], v3(xr)[:, :, 32:64]
            tt = lambda out, i0, i1, op, rd, wr: self.run(
                self.dve, lambda e: e.tensor_tensor(out=out, in0=i0, in1=i1, op=op), reads=rd, writes=wr)
            cb = [b["consts"]]
            tt(ra, x1, cosb, ALU.mult, [sc.bA2] + cb, [rba])
            tt(rb, x2, sinb, ALU.mult, [sc.bA2] + cb, [rbb])
            yield
            tt(o1, ra, rb, ALU.subtract, [rba, rbb], [sc.bA1])
            yield
            tt(ra, x2, cosb, ALU.mult, [sc.bA2] + cb, [rba])
            tt(rb, x1, sinb, ALU.mult, [sc.bA2] + cb, [rbb])
            yield
            tt(o2, ra, rb, ALU.add, [rba, rbb], [sc.bA1])
            yield
            fin, finb = xr, sc.bA1
        for (oap, wb) in outs:
            i0 = v3(fin)
            i1 = rs.unsqueeze(2).to_broadcast([128, nh, HD])
            self.run(self.dve, lambda e, oap=oap, i0=i0, i1=i1: e.tensor_tensor(out=oap, in0=i0, in1=i1, op=ALU.mult),
                     reads=[finb, sc.bst_q], writes=wb)
            yield

    def kt_store(self, sc, kdst, kbuf):
        ps_p, ps_t, bps, bpt = sc.ps_p, sc.ps_t, sc.bps, sc.bpt
        b = self.b
        for kp in range(2):
            self.run(self.pe, lambda e, kp=kp: e.transpose(ps_t[:, kp * 128:(kp + 1) * 128],
                                                      sc.kd[:, 2 * kp:2 * kp + 2, :].rearrange("p a d -> p (a d)"), self.ident[:]),
                     reads=[sc.bkd, b["consts"]], writes=[bpt], inc=(kp == 1))
        self.run(self.dve, lambda e: e.tensor_copy(out=kdst, in_=ps_t[:, 0:256].rearrange("p (k t) -> p k t", t=128)),
                 reads=[bpt], writes=[kbuf])

    def cached_kv(self, l):
        b = self.b
        for j in range(2):
            sc = self.sc[j]
            ps_p, ps_t, bps, bpt = sc.ps_p, sc.ps_t, sc.bps, sc.bpt
            kb = NLT + j
            hb_, hbuf = sc.bhB, sc.hB
            self.dma(self.sp, hbuf[:, 0:256], self.ck.ap()[l, j * 128:(j + 1) * 128, :], sc.khB, writes=[hb_])
            self.dma(self.sp, hbuf[:, 256:512], self.cv.ap()[l, j * 128:(j + 1) * 128, :], sc.khB, writes=[hb_])
            self.run(self.dve, lambda e, hbuf=hbuf, sc=sc: e.tensor_copy(out=sc.kd[:, :, :].rearrange("p h d -> p (h d)"), in_=hbuf[:, 0:256]),
                     reads=[hb_], writes=[sc.bkd])
            self.kt_store(sc, self.KT[:, :, kb * 128:(kb + 1) * 128], self.bKT[kb])
            self.run(self.act, lambda e, kb=kb, hbuf=hbuf: e.activation(out=self.VV[:, kb, :, :].rearrange("p h d -> p (h d)"),
                                                                       in_=hbuf[:, 256:512], func=AF.Copy),
                     reads=[hb_], writes=[self.bVV[kb]])

    def phase_a_tile_g(self, l, t, ctx, sc):
        ps_p, ps_t, bps, bpt = sc.ps_p, sc.ps_t, sc.bps, sc.bpt
        b = self.b
        kind = l % 3
        hbuf, hB = sc.hB, sc.bhB
        src, hd = self._hsrc(l, t, ctx)
        self.dma(self.sp, hbuf[:], src, sc.khB, reads=[hd], writes=[hB])
        yield
        yield from self.ln_u_g(sc, hbuf, hB)
        pk = ps_p[:, 0:512]
        for kc in range(8):
            self.run(self.pe, lambda e, kc=kc: e.matmul(pk, lhsT=sc.uT[:, kc, :], rhs=self.win[:, kc, 1024:1536],
                                                   start=(kc == 0), stop=(kc == 7)),
                     reads=[sc.buT, b["win"]], writes=[bps[0]], inc=(kc == 7))
        yield
        rope = (not ctx) and kind != 2
        outs = [(sc.kd[:], [sc.bkd])]
        kv32 = sc.R3[:, 0:512]
        if ctx:
            outs.append((kv32[:, 0:256].rearrange("p (h d) -> p h d", d=HD), [sc.bR3a]))
        yield from self.prep_heads_g(sc, ps_p[:, 0:256], NKV, self.gk, t if rope else None, [bps[0]], outs)
        self.kt_store(sc, self.KT[:, :, t * 128:(t + 1) * 128], self.bKT[t])
        yield
        self.run(self.act, lambda e: e.activation(out=self.VV[:, t, :, :].rearrange("p h d -> p (h d)"), in_=ps_p[:, 256:512], func=AF.Copy),
                 reads=[bps[0]], writes=[self.bVV[t]])
        yield
        if ctx:
            self.run(self.act, lambda e: e.activation(out=kv32[:, 256:512], in_=ps_p[:, 256:512], func=AF.Copy),
                     reads=[bps[0]], writes=[sc.bR3a])
            s_, r0 = t // 2, (t % 2) * 128
            self.dma(self.sp, self.nk.ap()[s_, l, r0:r0 + 128, :], kv32[:, 0:256], sc.kR3, reads=[sc.bR3a], writes=[self.bout])
            self.dma(self.sp, self.nv.ap()[s_, l, r0:r0 + 128, :], kv32[:, 256:512], sc.kR3, reads=[sc.bR3a], writes=[self.bout])
            yield

    def phase_a(self, l, ntiles, ctx):
        self.use_psum(1, own=True)
        for t in range(0, ntiles, 2):
            gens = [self.phase_a_tile_g(l, t, ctx, self.sc[0]), self.phase_a_tile_g(l, t + 1, ctx, self.sc[1])]
            for _ in _round_robin(gens):
                pass

    def _hsrc(self, l, t, ctx):
        if ctx:
            return (self.xp if l == 0 else self.hp_d).ap()[t * 128:(t + 1) * 128, :], self.bhp[t]
        return (self.xs if l == 0 else self.hs_d).ap()[t * 128:(t + 1) * 128, :], self.bhs[t]

    def stage1_tile_g(self, l, c, j, ctx, sc):
        ps_p, ps_t, bps, bpt = sc.ps_p, sc.ps_t, sc.bps, sc.bpt
        b = self.b
        kind = l % 3
        si = c % 2
        QT, QTb = self.QTs[si], b["QT%d" % si]
        zsT, zsTb = self.zsTs[si], b["zsT%d" % si]
        ppb = bps
        t = 2 * c + j
        hbuf, hB = sc.hB, sc.bhB
        src, hd = self._hsrc(l, t, ctx)
        self.dma(self.sp, hbuf[:], src, sc.khB, reads=[hd], writes=[hB])
        yield
        yield from self.ln_u_g(sc, hbuf, hB)
        for (c0, isq) in ((0, True), (1536, False)):
            for n in range(2):
                for kc in range(8):
                    self.run(self.pe, lambda e, kc=kc, n=n, c0=c0: e.matmul(
                        ps_p[:, n * 512:(n + 1) * 512], lhsT=sc.uT[:, kc, :],
                        rhs=self.win[:, kc, c0 + n * 512:c0 + (n + 1) * 512], start=(kc == 0), stop=(kc == 7)),
                        reads=[sc.buT, b["win"]], writes=[ppb[n]], inc=(kc == 7))
                yield
            if isq:
                rope = (not ctx) and kind != 2
                qr = sc.R4[:, 0:512].bitcast(BF16)
                srcb = [sc.bR4a]
                yield from self.prep_heads_g(sc, ps_p[:, :], NH, self.gq, t if rope else None, ppb,
                                             [(qr.rearrange("p (h d) -> p h d", d=HD), srcb)])
                srcT = qr
            else:
                zs = sc.R4[:, 512:1024].bitcast(BF16)
                srcb = [sc.bR4b]
                self.run(self.act, lambda e: e.activation(out=sc.A1[:], in_=ps_p[:, :], func=AF.Tanh, scale=0.5),
                         reads=ppb, writes=[sc.bA1])
                yield
                self.run(self.dve, lambda e, zs=zs: e.scalar_tensor_tensor(out=zs, in0=sc.A1[:], scalar=1.0, in1=ps_p[:, :],
                                                                      op0=ALU.add, op1=ALU.mult),
                         reads=ppb + [sc.bA1], writes=srcb)
                yield
                srcT = zs
            for pr in range(8):
                self.run(self.pe, lambda e, pr=pr, srcT=srcT: e.transpose(ps_t[:, pr * 128:(pr + 1) * 128],
                                                                     srcT[:, pr * 128:(pr + 1) * 128], self.ident[:]),
                         reads=srcb + [b["consts"]], writes=[bpt], inc=(pr == 7))
            yield
            if isq:
                for par in range(2):
                    qo = QT[par * 64:(par + 1) * 64, :, :].rearrange("p (pr two) q -> p pr two q", two=2)[:, :, par, j * 128:(j + 1) * 128]
                    self.run(self.dve, lambda e, qo=qo, par=par: e.tensor_copy(
                        out=qo, in_=ps_t[par * 64:(par + 1) * 64, :].rearrange("p (k t) -> p k t", t=128)),
                        reads=[bpt], writes=[QTb])
            else:
                self.run(self.dve, lambda e: e.tensor_copy(out=zsT[:, :, j * 128:(j + 1) * 128],
                                                          in_=ps_t[:, :].rearrange("p (k t) -> p k t", t=128)),
                         reads=[bpt], writes=[zsTb])
            yield

    def stage3_tile_g(self, l, c, j, ctx, sc):
        ps_p, ps_t, bps, bpt = sc.ps_p, sc.ps_t, sc.bps, sc.bpt
        b = self.b
        si = c % 2
        gT, gTb = self.zsTs[si], b["zsT%d" % si]
        last = (l == self.n_layers - 1)
        ppb = bps
        st = self.stat
        o = sc.so
        t = 2 * c + j
        hbuf, hB = sc.hR, sc.bhR
        src, hd = self._hsrc(l, t, ctx)
        self.dma(self.sp, hbuf[:], src, sc.khR, reads=[hd], writes=[hB])
        yield
        for n in range(2):
            for pr in range(8):
                self.run(self.pe, lambda e, pr=pr, n=n: e.matmul(
                    ps_p[:, n * 512:(n + 1) * 512], lhsT=gT[:, pr, j * 128:(j + 1) * 128],
                    rhs=self.wout[:, pr, n * 512:(n + 1) * 512], start=(pr == 0), stop=(pr == 7)),
                    reads=[gTb, b["wout"]], writes=[ppb[n]], inc=(pr == 7))
            yield
        self.run(self.act, lambda e: e.activation(out=sc.A1[:], in_=ps_p[:, :], func=AF.Square, accum_out=st[:, o + 80:o + 81]),
                 reads=ppb, writes=[sc.bA1, sc.bst_y])
        yield
        self.rstd_small(st[:, o + 80:o + 81], st[:, o + 81:o + 82], st[:, o + 82:o + 83], 1, 1.0 / D, [sc.bst_y])
        yield
        self.run(self.dve, lambda e: e.scalar_tensor_tensor(out=sc.A1[:], in0=ps_p[:, :], scalar=st[:, o + 82:o + 83],
                                                            in1=self.mods[:, 2 * D:3 * D], op0=ALU.mult, op1=ALU.mult),
                 reads=ppb + [sc.bst_y, b["mods"]], writes=[sc.bA1])
        yield
        self.run(self.dve, lambda e: e.tensor_tensor(out=sc.A2[:], in0=sc.A1[:], in1=hbuf[:], op=ALU.add),
                 reads=[sc.bA1, hB], writes=[sc.bA2])
        yield
        if ctx:
            dst = (self.yp if last else self.hp_d).ap()[t * 128:(t + 1) * 128, :]
            hd = self.bhp[t]
        else:
            dst = (self.ys if last else self.hs_d).ap()[t * 128:(t + 1) * 128, :]
            hd = self.bhs[t]
        self.dma(self.sp, dst, sc.A2[:], sc.kA2, reads=[sc.bA2], writes=[hd])
        yield

    def phase_b(self, l, nch, ctx):
        import itertools
        kind = l % 3
        dual = (kind != 0)

        def fill(c3, c1):
            chains = []
            for j in range(2):
                g = []
                if c3 is not None:
                    g.append(self.stage3_tile_g(l, c3, j, ctx, self.sc[j]))
                if c1 is not None:
                    g.append(self.stage1_tile_g(l, c1, j, ctx, self.sc[j]))
                chains.append(itertools.chain(*g))
            return _round_robin(chains) if dual else itertools.chain(*chains)

        self.use_psum(1, own=True)
        for _ in _round_robin([self.stage1_tile_g(l, 0, j, ctx, self.sc[j]) for j in range(2)]):
            pass
        self.use_psum(1, own=dual)
        for c in range(nch):
            filler = fill(c - 1 if c > 0 else None, c + 1 if c + 1 < nch else None)
            self.attention(l, c, ctx, filler)
            for _ in filler:
                pass
        self.use_psum(1, own=True)
        for _ in _round_robin([self.stage3_tile_g(l, nch - 1, j, ctx, self.sc[j]) for j in range(2)]):
            pass

    def attention(self, l, c, ctx, filler=None):
        b, ds = self.b, self.ds
        si = c % 2
        QT, QTb = self.QTs[si], b["QT%d" % si]
        zsT, zsTb = self.zsTs[si], b["zsT%d" % si]
        kind = l % 3
        sink = (kind == 1)
        blocks = []
        if ctx:
            for j in range(2):
                kb = 2 * c + j
                blocks.append(("c", kb, None))
        else:
            if kind == 0:
                for kb in range(NLT):
                    blocks.append(("l", kb, None))
            elif kind == 1:
                for o in range(-1, 3):
                    kb = 2 * c + o
                    if 0 <= kb < NLT:
                        blocks.append(("l", kb, ("w", o + 1)))
            else:
                for o in range(-2, 4):
                    kb = 2 * c + o
                    if 0 <= kb < NLT:
                        blocks.append(("l", kb, ("n", 7 - 2 * o)))
            blocks.append(("l", NLT, None))
            blocks.append(("l", NLT + 1, None))
        import os
        if not ctx:
            blocks = blocks[:int(os.environ.get('DBG_NKB', '99'))]
        na = (not ctx) and kind == 2
        variant = 1 if (c == 0 or c == TL // NQ - 1) else 0
        if kind == 0:
            GS = 4
            Sap = [self.ps_s[0][:, :], self.ps_s[1][:, :]]
            psb = [[b["pb0"], b["pb1"]], [b["pb2"], b["pb3"]]]
        else:
            GS = 2
            Sap = [self.ps_s[1][:, 0:512], self.ps_s[1][:, 512:1024]]
            psb = [[b["pb2"]], [b["pb3"]]]
        groups = [blocks[i:i + GS] for i in range(0, len(blocks), GS)]
        units = []
        for h in range(NH if ctx else int(os.environ.get('DBG_NHD', '16'))):
            for gi, g in enumerate(groups):
                units.append((h, gi, g, gi == 0, gi == len(groups) - 1))
        P = [self.A3[:, 0:512].bitcast(BF16), self.A3[:, 512:1024].bitcast(BF16), self.P3[:, :]]
        Pb = [b["A3a"], b["A3b"], b["P3"]]
        ob = [b["ps_o0"], b["ps_o1"]]
        state = {"acc_started": {}}

        def kt_ap(kindb, kb, kv, base):
            if kindb == "c":
                return self.KTc[:, kv // 2, kb * 128:(kb + 1) * 128], self.bKTc[kb]
            return self.KT[:, kv // 2, kb * 128:(kb + 1) * 128], self.bKT[kb]

        def v_ap(kindb, kb, kv):
            if kindb == "c":
                return self.VVc[:, kb, kv, :], self.bVVc[kb]
            return self.VV[:, kb, kv, :], self.bVV[kb]

        def emit_qk(ui):
            h, gi, g, first, lastg = units[ui]
            kv, base = PERM[h] // 4, (h % 2) * 64
            S, Sb = Sap[ui % 2], psb[ui % 2]
            if na and gi == 0:
                ti = h % 2
                self.dma(self.sp, self.tbl[ti][:, 0:960], self.nab.ap()[variant, PERM[h]], "tbl%d" % ti, reads=[b["nab"]], writes=[b["tbl%d" % ti]])
            for j, (kb_kind, kb, extra) in enumerate(g):
                kt, ktb = kt_ap(kb_kind, kb, kv, base)
                out = S[:, j * NQ:(j + 1) * NQ]
                lastmm = (j == len(g) - 1)
                self.run(self.pe, lambda e, out=out, kt=kt, h=h, base=base, extra=extra: e.matmul(
                    out, lhsT=kt, rhs=QT[:, h, :], start=True, stop=(extra is None)),
                    reads=[ktb, QTb], writes=Sb, inc=(lastmm and extra is None))
                if extra is not None:
                    if extra[0] == "w":
                        ex, exb = self.wmask[:, extra[1], :], b["tbl0"]
                    else:
                        ti = h % 2
                        ex, exb = self.tbl[ti][:, extra[1] * 64:(extra[1] + 4) * 64], b["tbl%d" % ti]
                    self.run(self.pe, lambda e, out=out, ex=ex: e.matmul(out, lhsT=self.ident[:], rhs=ex, start=False, stop=True),
                             reads=[exb, b["consts"]], writes=Sb, inc=lastmm)

        def emit_exp(ui):
            h, gi, g, first, lastg = units[ui]
            n = len(g) * NQ
            S, Sb = Sap[ui % 2], psb[ui % 2]
            self.run(self.act, lambda e, S=S, n=n, ui=ui: e.activation(out=P[ui % 3][:, 0:n], in_=S[:, 0:n], func=AF.Exp),
                     reads=Sb, writes=[Pb[ui % 3]])

        def emit_pv(ui):
            h, gi, g, first, lastg = units[ui]
            kv = PERM[h] // 4
            O = self.ps_oo[h % 2][:, 0:NQ]
            Ob = ob[h % 2]
            for j, (kb_kind, kb, extra) in enumerate(g):
                va, vb = v_ap(kb_kind, kb, kv)
                rhs = P[ui % 3][:, j * NQ:(j + 1) * NQ]
                st_ = first and j == 0
                sp_ = lastg and j == len(g) - 1
                self.run(self.pe, lambda e, O=O, va=va, rhs=rhs, st_=st_, sp_=sp_: e.matmul(O[0:64, :], lhsT=va, rhs=rhs, start=st_, stop=sp_),
                         reads=[vb, Pb[ui % 3]], writes=[Ob], inc=False)
                self.run(self.pe, lambda e, O=O, rhs=rhs, st_=st_, sp_=sp_: e.matmul(O[64:128, :], lhsT=self.ones64[:, :], rhs=rhs, start=st_, stop=sp_),
                         reads=[b["consts"], Pb[ui % 3]], writes=[Ob], inc=(j == len(g) - 1))

        def emit_post(h):
            base = (h % 2) * 64
            O = self.ps_oo[h % 2][:, 0:NQ]
            Ob = ob[h % 2]
            lnd = self.A4[64:128, 0:NQ]
            rden = self.A4[64:128, NQ:2 * NQ]
            tt = self.A4[base:base + 64, (2 + h % 2) * NQ:(3 + h % 2) * NQ]
            ttb = b["A4c"] if h % 2 == 0 else b["A4d"]
            if kind == 1:
                if sink:
                    self.run(self.act, lambda e: e.activation(out=lnd, in_=O[64:128, :], func=AF.Ln, bias=self.esink[64:128, PERM[h]:PERM[h] + 1]),
                             reads=[Ob, b["esink"]], writes=[b["A4a"]])
                else:
                    self.run(self.act, lambda e: e.activation(out=lnd, in_=O[64:128, :], func=AF.Ln), reads=[Ob], writes=[b["A4a"]])
                self.run(self.act, lambda e: e.activation(out=rden, in_=lnd, func=AF.Exp, scale=-1.0), reads=[b["A4a"]], writes=[b["A4b"]])
            elif sink:
                self.run(self.dve, lambda e: e.tensor_scalar(lnd, O[64:128, :], self.esink[64:128, PERM[h]:PERM[h] + 1], None, ALU.add),
                         reads=[Ob, b["esink"]], writes=[b["A4a"]])
                self.run(self.dve, lambda e: e.reciprocal(out=rden, in_=lnd), reads=[b["A4a"]], writes=[b["A4b"]])
            else:
                self.run(self.dve, lambda e: e.reciprocal(out=rden, in_=O[64:128, :]), reads=[Ob], writes=[b["A4b"]])
            self.run(self.dve, lambda e: e.scalar_tensor_tensor(out=tt, in0=O[0:64, :], scalar=0.5, in1=rden, op0=ALU.mult, op1=ALU.mult),
                     reads=[Ob, b["A4b"]], writes=[ttb])
            self.run(self.pool, lambda e: e.tensor_tensor(out=zsT[base:base + 64, h // 2, :], in0=tt,
                                                          in1=zsT[base:base + 64, h // 2, :], op=ALU.mult),
                     reads=[ttb, zsTb], writes=[zsTb])

        nU = len(units)
        if os.environ.get('DBG_NOPIPE'):
            for ui in range(nU):
                emit_qk(ui)
                emit_exp(ui)
                emit_pv(ui)
                if units[ui][4]:
                    emit_post(units[ui][0])
            return
        emit_qk(0)
        if nU > 1:
            emit_qk(1)
        pend = None
        import math
        k_fill = max(1, math.ceil(90.0 / nU))
        for ui in range(nU):
            if filler is not None and ui >= 2:
                for _ in range(k_fill):
                    if next(filler, "done") == "done":
                        filler = None
                        break
            emit_exp(ui)
            if pend is not None:
                emit_post(pend)
                pend = None
            if ui + 2 < nU:
                emit_qk(ui + 2)
            emit_pv(ui)
            if units[ui][4]:
                pend = units[ui][0]
        emit_post(pend)


_CACHE = {}


def _constants():
    t = np.arange(TL, dtype=np.int64)
    row = (t // GRID_W).astype(np.float32)
    col = (t % GRID_W).astype(np.float32)
    inv = (np.float32(10000.0) ** (-np.arange(16, dtype=np.float32) / np.float32(16))).astype(np.float32)
    ang = np.concatenate([row[:, None] * inv, col[:, None] * inv], axis=-1).astype(np.float32)
    cos = np.cos(ang).astype(np.float32).reshape(NLT, 128, 32).transpose(1, 0, 2).copy()
    sin = np.sin(ang).astype(np.float32).reshape(NLT, 128, 32).transpose(1, 0, 2).copy()
    ident = np.eye(128, dtype=np.float32)
    wm = np.full((4, 128, NQ), NEG, dtype=np.float32)
    kk = np.arange(128)[:, None]
    qq = np.arange(128)[None, :]
    for oi, o in enumerate(range(-1, 3)):
        for j in range(2):
            d = o - j
            if d == 0:
                m = np.zeros((128, 128), np.float32)
            elif d == -1:
                m = np.where(qq <= kk, 0.0, NEG).astype(np.float32)
            elif d == 1:
                m = np.where(kk <= qq, 0.0, NEG).astype(np.float32)
            else:
                continue
            wm[oi, :, j * 128:(j + 1) * 128] = m
    cm = np.zeros((2, 128, 15, 64), dtype=np.float32)
    cc = np.arange(64)
    cs = np.clip(cc - 8, 0, 48)
    for half in range(2):
        for kc in range(64):
            p = half * 64 + kc
            colok = (kc >= cs) & (kc < cs + 16)
            for s in range(15):
                delta = (7 - s) if half == 0 else (8 - s)
                for v in range(2):
                    rowok = (-4 <= delta <= 3) if v == 0 else (-7 <= delta <= 7)
                    cm[v, p, s, :] = np.where(colok & rowok, 0.0, NEG)
    return cos, sin, ident, wm, cm.reshape(2, 128, 15 * 64)


def _tsrc(na_rel_bias):
    tab = np.asarray(na_rel_bias, dtype=np.float32)[0]
    out = np.zeros((2, NH, 15, 128), dtype=np.float32)
    rev = tab[:, :, ::-1]
    for s in range(15):
        dr0 = 14 - s
        out[0, :, s, 48:79] = rev[:, dr0, :]
        dr1 = 15 - s
        if dr1 <= 14:
            out[1, :, s, 48:79] = rev[:, dr1, :]
    return out


def _perm_w_in(w):
    idx = np.concatenate([np.arange(64) + 64 * h for h in PERM])
    cols = np.concatenate([idx, np.arange(1024, 1536), 1536 + idx])
    return np.ascontiguousarray(w[:, :, cols])


def _perm_w_out(w):
    idx = np.concatenate([np.arange(64) + 64 * h for h in PERM])
    return np.ascontiguousarray(w[:, idx, :])


def _get_prog(n_layers=NL):
    if n_layers not in _CACHE:
        p = Prog(n_layers)
        _CACHE[n_layers] = p.build()
    return _CACHE[n_layers]


def make_in_maps(inputs, n_cores=8):
    f = lambda a: np.ascontiguousarray(np.asarray(a, dtype=np.float32))
    cos, sin, ident, wm, cm = _constants()
    tsrc = _tsrc(inputs["na_rel_bias"])
    shared = {
        "w_mod": f(inputs["w_mod"]), "b_mod": f(inputs["b_mod"]), "norm_pre": f(inputs["norm_pre"]),
        "norm_post": f(inputs["norm_post"]), "w_in": _perm_w_in(f(inputs["w_in"])), "q_norm": f(inputs["q_norm"]),
        "k_norm": f(inputs["k_norm"]), "w_out": _perm_w_out(f(inputs["w_out"])), "sink": f(inputs["sink_logit"]).reshape(1, NH),
        "tsrc": tsrc, "cos_t": cos, "sin_t": sin, "ident_f": ident, "wmask_f": wm, "cmask_f": cm,
    }
    xs, xp = f(inputs["x_sample"]), f(inputs["x_prompt"])
    ck, cv = f(inputs["cache_k"]), f(inputs["cache_v"])
    c, cctx = f(inputs["c"]), f(inputs["c_ctx"])
    maps = []
    for i in range(n_cores):
        m = dict(shared)
        m["xs"] = xs[i]
        m["xp"] = xp[2 * i:2 * i + 2].reshape(TCX, D)
        m["ck"] = ck[i].reshape(NL, 256, 256)
        m["cv"] = cv[i].reshape(NL, 256, 256)
        m["cond"] = np.stack([c[i], cctx], axis=0)
        maps.append(m)
    return maps


def kernel(x_prompt, x_sample, cache_k, cache_v, c, c_ctx, w_mod, b_mod, norm_pre, norm_post,
           w_in, q_norm, k_norm, w_out, sink_logit, na_rel_bias):
    inputs = dict(x_prompt=x_prompt, x_sample=x_sample, cache_k=cache_k, cache_v=cache_v, c=c, c_ctx=c_ctx,
                  w_mod=w_mod, b_mod=b_mod, norm_pre=norm_pre, norm_post=norm_post, w_in=w_in, q_norm=q_norm,
                  k_norm=k_norm, w_out=w_out, sink_logit=sink_logit, na_rel_bias=na_rel_bias)
    nc = _get_prog(NL)
    maps = make_in_maps(inputs, 8)
    res = run_bass_kernel_spmd(nc, maps, core_ids=list(range(8)))
    r = res.results
    y_prompt = np.concatenate([r[i]["yp"].reshape(2, 256, D) for i in range(8)], axis=0).astype(np.float32)
    y_sample = np.stack([r[i]["ys"] for i in range(8)], axis=0).astype(np.float32)
    nk = np.concatenate([r[i]["nk"].reshape(2, NL, 256, NKV, HD) for i in range(8)], axis=0).astype(np.float32)
    nv = np.concatenate([r[i]["nv"].reshape(2, NL, 256, NKV, HD) for i in range(8)], axis=0).astype(np.float32)
    return (y_prompt, y_sample, nk, nv)
```
